# Optimizing a Trainium2 kernel written in Bass

```python
import math
import jax
import jax.numpy as jnp
from jax import lax
import numpy as np

D_MODEL = 1024
BATCH = 16
SEQ = 4096
DEPTH = 1

MIX_WIDTH = D_MODEL
ATTN_WIDTH = MIX_WIDTH // 2
SSM_WIDTH = MIX_WIDTH - ATTN_WIDTH
ATTN_HEADS = 4
ATTN_HEAD_DIM = ATTN_WIDTH // (2 * ATTN_HEADS)
SSM_GROUP = 16
SSM_GROUPS = SSM_WIDTH // SSM_GROUP
SSM_STATE = 64
DT_MIN = 1e-3
DT_MAX = 1e-1
D_FF = 2816
CONV_WIDTH = 3
REL_BUCKETS = 32
REL_MAX_DIST = 128
Q_BLOCK = 128
LN_EPS = 1e-5
NEG_INF = -1e30
DEEPNORM_ALPHA = (2 * DEPTH) ** 0.25
DEEPNORM_BETA = (8 * DEPTH) ** -0.25

kernel_name = 'hybrid_diffattn_s5_convffn_deepnorm_adaln'


def _layernorm(x, g=None, b=None):
    xf = x.astype(jnp.float32)
    mu = jnp.mean(xf, axis=-1, keepdims=True)
    var = jnp.mean(jnp.square(xf - mu), axis=-1, keepdims=True)
    y = (xf - mu) * lax.rsqrt(var + LN_EPS)
    if g is not None:
        y = y * g.astype(jnp.float32) + b.astype(jnp.float32)
    return y.astype(x.dtype)


def _rmsnorm(x, g):
    xf = x.astype(jnp.float32)
    y = xf * lax.rsqrt(jnp.mean(jnp.square(xf), axis=-1, keepdims=True) + LN_EPS) * g.astype(jnp.float32)
    return y.astype(x.dtype)


def _t5_bucket(dist):
    max_exact = REL_BUCKETS // 2
    distf = jnp.maximum(dist, 1).astype(jnp.float32)
    large = max_exact + (jnp.log(distf / max_exact) / math.log(REL_MAX_DIST / max_exact)
                         * (REL_BUCKETS - max_exact)).astype(jnp.int32)
    large = jnp.minimum(large, REL_BUCKETS - 1)
    return jnp.where(dist < max_exact, dist, large)


def _diff_attention(q, k, v, lam, rel_bias):
    seq = q.shape[3]
    scale = ATTN_HEAD_DIM ** -0.5
    table = rel_bias.astype(jnp.float32)
    outs = []
    for start in range(0, seq, Q_BLOCK):
        end = start + Q_BLOCK
        qb = q[:, :, :, start:end]
        kb = k[:, :, :, :end]
        vb = v[:, :, :end]
        logits = jnp.einsum('bhmqd,bhmkd->bhmqk', qb, kb).astype(jnp.float32) * scale
        dist = jnp.arange(start, end)[:, None] - jnp.arange(end)[None, :]
        bias = jnp.transpose(table[_t5_bucket(jnp.maximum(dist, 0))], (2, 0, 1))
        logits = logits + bias[None, :, None]
        logits = jnp.where(dist >= 0, logits, NEG_INF)
        p = jax.nn.softmax(logits, axis=-1)
        w = p[:, :, 0] - lam * p[:, :, 1]
        outs.append(jnp.einsum('bhqk,bhke->bhqe', w.astype(v.dtype), vb))
    return jnp.concatenate(outs, axis=2)


def _complex_affine_combine(e1, e2):
    a1r, a1i, b1r, b1i = e1
    a2r, a2i, b2r, b2i = e2
    ar = a2r * a1r - a2i * a1i
    ai = a2r * a1i + a2i * a1r
    br = a2r * b1r - a2i * b1i + b2r
    bi = a2r * b1i + a2i * b1r + b2i
    return (ar, ai, br, bi)


def _s5_ssm(s, a_re, a_im, b_re, b_im, c_re, c_im, d_skip, log_dt):
    f32 = jnp.float32
    bsz, seq, _ = s.shape
    sf = s.astype(f32)
    ug = sf.reshape(bsz, seq, SSM_GROUPS, SSM_GROUP)
    dt = jnp.exp(log_dt.astype(f32))[:, None]
    ar = a_re.astype(f32)
    ai = a_im.astype(f32)
    mag = jnp.exp(ar * dt)
    ang = ai * dt
    abar_re = mag * jnp.cos(ang)
    abar_im = mag * jnp.sin(ang)
    den = ar * ar + ai * ai
    fr = ((abar_re - 1.0) * ar + abar_im * ai) / den
    fi = (abar_im * ar - (abar_re - 1.0) * ai) / den
    br = b_re.astype(f32)
    bi = b_im.astype(f32)
    bbar_re = fr[..., None] * br - fi[..., None] * bi
    bbar_im = fr[..., None] * bi + fi[..., None] * br
    bu_re = jnp.einsum('blgh,gph->blgp', ug, bbar_re)
    bu_im = jnp.einsum('blgh,gph->blgp', ug, bbar_im)
    a_seq_re = jnp.broadcast_to(abar_re, (1, seq, SSM_GROUPS, SSM_STATE))
    a_seq_im = jnp.broadcast_to(abar_im, (1, seq, SSM_GROUPS, SSM_STATE))
    _, _, h_re, h_im = lax.associative_scan(
        _complex_affine_combine, (a_seq_re, a_seq_im, bu_re, bu_im), axis=1)
    y = (jnp.einsum('blgp,ghp->blgh', h_re, c_re.astype(f32))
         - jnp.einsum('blgp,ghp->blgh', h_im, c_im.astype(f32)))
    return y.reshape(bsz, seq, SSM_WIDTH) + d_skip.astype(f32) * sf


def _hybrid_mixer(u, w_in, lam_q1, lam_k1, lam_q2, lam_k2, subln_g,
                  ssm_a_re, ssm_a_im, ssm_b_re, ssm_b_im, ssm_c_re, ssm_c_im,
                  ssm_d, ssm_log_dt, w_glu, b_glu, w_out, rel_bias, lam_init):
    f32 = jnp.float32
    bsz, seq, _ = u.shape
    proj = u @ w_in
    q, k, v, s = jnp.split(proj, [ATTN_WIDTH, 2 * ATTN_WIDTH, 3 * ATTN_WIDTH], axis=-1)
    q = q.reshape(bsz, seq, ATTN_HEADS, 2, ATTN_HEAD_DIM).transpose(0, 2, 3, 1, 4)
    k = k.reshape(bsz, seq, ATTN_HEADS, 2, ATTN_HEAD_DIM).transpose(0, 2, 3, 1, 4)
    v = v.reshape(bsz, seq, ATTN_HEADS, 2 * ATTN_HEAD_DIM).transpose(0, 2, 1, 3)
    lam = (jnp.exp(jnp.sum(lam_q1.astype(f32) * lam_k1.astype(f32)))
           - jnp.exp(jnp.sum(lam_q2.astype(f32) * lam_k2.astype(f32))) + lam_init)
    o = _diff_attention(q, k, v, lam, rel_bias)
    o = _rmsnorm(o, subln_g) * (1.0 - lam_init)
    o = o.transpose(0, 2, 1, 3).reshape(bsz, seq, ATTN_WIDTH)
    y = jax.nn.gelu(_s5_ssm(s, ssm_a_re, ssm_a_im, ssm_b_re, ssm_b_im,
                            ssm_c_re, ssm_c_im, ssm_d, ssm_log_dt))
    y = (y * jax.nn.sigmoid(y @ w_glu.astype(f32) + b_glu.astype(f32))).astype(u.dtype)
    return jnp.concatenate([o, y], axis=-1) @ w_out


def _conv_ffn(u, w_up, conv_w, conv_b, w_down):
    h = u @ w_up
    ch = h.shape[-1]
    h = lax.conv_general_dilated(h, conv_w[:, None, :], window_strides=(1,),
                                 padding=[(CONV_WIDTH - 1, 0)],
                                 dimension_numbers=('NWC', 'WIO', 'NWC'),
                                 feature_group_count=ch) + conv_b
    val, gate = jnp.split(h, 2, axis=-1)
    return (jax.nn.silu(gate) * val) @ w_down


def setup_inputs(seed: int = 0) -> dict:
    key = jax.random.key(seed)
    ks = jax.random.split(key, 32)
    f32 = jnp.float32

    def nrm(k, shape, std):
        return jax.random.normal(k, shape, f32) * std

    x = nrm(ks[0], (BATCH, SEQ, D_MODEL), 1.0)
    c = nrm(ks[1], (BATCH, D_MODEL), 1.0)
    w_ada = nrm(ks[2], (DEPTH, D_MODEL, 6 * D_MODEL), 0.5 * D_MODEL ** -0.5)
    b_ada = nrm(ks[3], (DEPTH, 6 * D_MODEL), 0.02)
    w_qk = nrm(ks[4], (DEPTH, D_MODEL, 2 * ATTN_WIDTH), D_MODEL ** -0.5)
    w_v = nrm(ks[5], (DEPTH, D_MODEL, ATTN_WIDTH), DEEPNORM_BETA * D_MODEL ** -0.5)
    w_s = nrm(ks[6], (DEPTH, D_MODEL, SSM_WIDTH), D_MODEL ** -0.5)
    w_in = jnp.concatenate([w_qk, w_v, w_s], axis=-1)
    lam_q1 = nrm(ks[7], (DEPTH, ATTN_HEAD_DIM), 0.1)
    lam_k1 = nrm(ks[8], (DEPTH, ATTN_HEAD_DIM), 0.1)
    lam_q2 = nrm(ks[9], (DEPTH, ATTN_HEAD_DIM), 0.1)
    lam_k2 = nrm(ks[10], (DEPTH, ATTN_HEAD_DIM), 0.1)
    subln_g = 1.0 + nrm(ks[11], (DEPTH, 2 * ATTN_HEAD_DIM), 0.02)
    ssm_a_re = -0.5 * jnp.exp(nrm(ks[12], (DEPTH, SSM_GROUPS, SSM_STATE), 0.01))
    ssm_a_im = (math.pi * jnp.arange(SSM_STATE, dtype=f32)[None, None, :]
                + nrm(ks[13], (DEPTH, SSM_GROUPS, SSM_STATE), 0.01))
    ssm_b_re = nrm(ks[14], (DEPTH, SSM_GROUPS, SSM_STATE, SSM_GROUP), (2 * SSM_GROUP) ** -0.5)
    ssm_b_im = nrm(ks[15], (DEPTH, SSM_GROUPS, SSM_STATE, SSM_GROUP), (2 * SSM_GROUP) ** -0.5)
    ssm_c_re = nrm(ks[16], (DEPTH, SSM_GROUPS, SSM_GROUP, SSM_STATE), (2 * SSM_STATE) ** -0.5)
    ssm_c_im = nrm(ks[17], (DEPTH, SSM_GROUPS, SSM_GROUP, SSM_STATE), (2 * SSM_STATE) ** -0.5)
    ssm_d = nrm(ks[18], (DEPTH, SSM_WIDTH), 1.0)
    ssm_log_dt = jax.random.uniform(ks[19], (DEPTH, SSM_GROUPS), f32,
                                    math.log(DT_MIN), math.log(DT_MAX))
    w_glu = nrm(ks[20], (DEPTH, SSM_WIDTH, SSM_WIDTH), SSM_WIDTH ** -0.5)
    b_glu = nrm(ks[21], (DEPTH, SSM_WIDTH), 0.02)
    w_out = nrm(ks[22], (DEPTH, MIX_WIDTH, D_MODEL), DEEPNORM_BETA * MIX_WIDTH ** -0.5)
    ln1_g = 1.0 + nrm(ks[23], (DEPTH, D_MODEL), 0.02)
    ln1_b = nrm(ks[24], (DEPTH, D_MODEL), 0.02)
    w_up = nrm(ks[25], (DEPTH, D_MODEL, 2 * D_FF), DEEPNORM_BETA * D_MODEL ** -0.5)
    conv_w = nrm(ks[26], (DEPTH, CONV_WIDTH, 2 * D_FF), CONV_WIDTH ** -0.5)
    conv_b = nrm(ks[27], (DEPTH, 2 * D_FF), 0.02)
    w_down = nrm(ks[28], (DEPTH, D_FF, D_MODEL), DEEPNORM_BETA * D_FF ** -0.5)
    ln2_g = 1.0 + nrm(ks[29], (DEPTH, D_MODEL), 0.02)
    ln2_b = nrm(ks[30], (DEPTH, D_MODEL), 0.02)
    rel_bias = nrm(ks[31], (REL_BUCKETS, ATTN_HEADS), 0.3)
    return {'x': x, 'c': c, 'w_ada': w_ada, 'b_ada': b_ada, 'w_in': w_in,
            'lam_q1': lam_q1, 'lam_k1': lam_k1, 'lam_q2': lam_q2, 'lam_k2': lam_k2,
            'subln_g': subln_g, 'ssm_a_re': ssm_a_re, 'ssm_a_im': ssm_a_im,
            'ssm_b_re': ssm_b_re, 'ssm_b_im': ssm_b_im, 'ssm_c_re': ssm_c_re,
            'ssm_c_im': ssm_c_im, 'ssm_d': ssm_d, 'ssm_log_dt': ssm_log_dt,
            'w_glu': w_glu, 'b_glu': b_glu, 'w_out': w_out, 'ln1_g': ln1_g, 'ln1_b': ln1_b,
            'w_up': w_up, 'conv_w': conv_w, 'conv_b': conv_b, 'w_down': w_down,
            'ln2_g': ln2_g, 'ln2_b': ln2_b, 'rel_bias': rel_bias}


def reference(x, c, w_ada, b_ada, w_in, lam_q1, lam_k1, lam_q2, lam_k2, subln_g,
              ssm_a_re, ssm_a_im, ssm_b_re, ssm_b_im, ssm_c_re, ssm_c_im, ssm_d,
              ssm_log_dt, w_glu, b_glu, w_out, ln1_g, ln1_b, w_up, conv_w, conv_b,
              w_down, ln2_g, ln2_b, rel_bias):
    c_act = jax.nn.silu(c)
    for layer in range(DEPTH):
        lam_init = 0.8 - 0.6 * math.exp(-0.3 * layer)
        mod = c_act @ w_ada[layer] + b_ada[layer]
        shift1, scale1, gate1, shift2, scale2, gate2 = [
            m[:, None, :] for m in jnp.split(mod, 6, axis=-1)]
        u = _layernorm(x) * (1.0 + scale1) + shift1
        m = _hybrid_mixer(u, w_in[layer], lam_q1[layer], lam_k1[layer], lam_q2[layer],
                          lam_k2[layer], subln_g[layer], ssm_a_re[layer], ssm_a_im[layer],
                          ssm_b_re[layer], ssm_b_im[layer], ssm_c_re[layer], ssm_c_im[layer],
                          ssm_d[layer], ssm_log_dt[layer], w_glu[layer], b_glu[layer],
                          w_out[layer], rel_bias, lam_init)
        x = _layernorm(DEEPNORM_ALPHA * x + gate1 * m, ln1_g[layer], ln1_b[layer])
        u = _layernorm(x) * (1.0 + scale2) + shift2
        f = _conv_ffn(u, w_up[layer], conv_w[layer], conv_b[layer], w_down[layer])
        x = _layernorm(DEEPNORM_ALPHA * x + gate2 * f, ln2_g[layer], ln2_b[layer])
    return x
```

```python
import math
from contextlib import ExitStack
import numpy as np
import concourse.bass as bass
import concourse.mybir as mybir
from concourse.bass_utils import run_bass_kernel_spmd

F32 = mybir.dt.float32
BF16 = mybir.dt.bfloat16
I32 = mybir.dt.int32
AF = mybir.ActivationFunctionType
ALU = mybir.AluOpType
AX = mybir.AxisListType

P = 128
D = 1024
NK = 8
DFF = 2816
NJ = 22
TT = 16
EPS = 1e-5
ALPHA = 2.0 ** 0.25
LAM_INIT = 0.2
NDS = 24
TWO_PI = 2.0 * math.pi


class _E:
    def __init__(self, name, h, sem):
        self.name, self.h, self.sem, self.count, self.waited = name, h, sem, 0, {}
        self.seq = 0
        self.cmap = {}


class FW:
    def __init__(self, nc, needed=None):
        self.nc = nc
        self.needed = needed
        self.rec = set()
        self.eng = {}
        self.sems = {}
        for name, h in [("pe", nc.tensor), ("dve", nc.vector), ("act", nc.scalar),
                        ("pool", nc.gpsimd), ("sp", nc.sync)]:
            s = nc.alloc_semaphore(name="sem_" + name)
            self.eng[name] = _E(name, h, s)
            self.sems[name] = s
        for i in range(NDS):
            self.sems["d%d" % i] = nc.alloc_semaphore(name="dsem%d" % i)
        self.dcount = 0
        self.lastw = {}
        self.readers = {}

    def _wait(self, e, tk):
        if tk is None:
            return
        key, val = tk
        if e.waited.get(key, 0) >= val:
            return
        if key in self.eng:
            self.rec.add((key, val))
            e.h.wait_ge(self.sems[key], self.eng[key].cmap[val])
        else:
            e.h.wait_ge(self.sems[key], val)
        e.waited[key] = val

    def _deps(self, e, reads, writes, same_engine_all):
        for r in reads:
            self._wait(e, self.lastw.get(r))
        for w in writes:
            t = self.lastw.get(w)
            if t is not None and (same_engine_all or t[0] != e.name):
                self._wait(e, t)
            for k, v in self.readers.get(w, {}).items():
                if same_engine_all or k != e.name:
                    self._wait(e, (k, v))

    def _record(self, tk, reads, writes):
        for r in reads:
            d = self.readers.setdefault(r, {})
            if d.get(tk[0], 0) < tk[1]:
                d[tk[0]] = tk[1]
        for w in writes:
            self.lastw[w] = tk
            self.readers[w] = {}

    def op(self, ename, fn, reads=(), writes=(), inc=True):
        e = self.eng[ename]
        self._deps(e, reads, writes, False)
        inst = fn(e.h)
        if inc:
            e.seq += 1
            tk = (e.name, e.seq)
            if self.needed is None or tk in self.needed:
                e.count += 1
                inst.then_inc(e.sem, 1)
                e.cmap[e.seq] = e.count
        else:
            tk = (e.name, e.seq + 1)
        self._record(tk, reads, writes)
        return inst

    def dma(self, qname, out, in_, reads=(), writes=()):
        e = self.eng[qname]
        i = self.dcount
        self.dcount += 1
        key = "d%d" % (i % NDS)
        val = 16 * (i // NDS + 1)
        if i >= NDS:
            self._wait(e, (key, val - 16))
        self._deps(e, reads, writes, True)
        e.h.dma_start(out=out, in_=in_).then_inc(self.sems[key], 16)
        self._record((key, val), reads, writes)

    def barrier(self):
        engs = list(self.eng.values())
        for e in engs:
            for f in engs:
                if f is not e and f.seq > 0:
                    self._wait(e, (f.name, f.seq))
            for i in range(min(self.dcount, NDS)):
                n = (self.dcount - 1 - i) // NDS + 1
                self._wait(e, ("d%d" % i, 16 * n))
        self.lastw = {}
        self.readers = {}


def _t5_thresholds():
    d = np.arange(0, 300, dtype=np.int32)
    df = np.maximum(d, 1).astype(np.float32)
    large = 16 + (np.log(df / np.float32(16)) / np.float32(math.log(128 / 16)) * np.float32(16)).astype(np.int32)
    large = np.minimum(large, 31)
    bucket = np.where(d < 16, d, large)
    return [int(np.argmax(bucket >= b)) for b in range(1, 32)]


def build(L, NSEQ, dbg=False):
    _, _, rec = _build(L, NSEQ, dbg, None)
    nc, dbg_outs, _ = _build(L, NSEQ, dbg, rec)
    return nc, dbg_outs


def _build(L, NSEQ, dbg, needed):
    NCH = L // 512
    NB = L // 128
    NC16 = L // TT
    nc = bass.Bass("TRN2", target_bir_lowering=False)
    fw = FW(nc, needed)
    dbg_outs = {}

    def din(name, shape, dt=F32):
        return nc.dram_tensor(name, list(shape), dt, kind="ExternalInput").ap()

    def dscr(name, shape, dt):
        return nc.dram_tensor(name, list(shape), dt, kind="Internal").ap()

    x_d = din("x", [NSEQ, L, D])
    cT_d = din("cT", [P, NSEQ, NK])
    wada_d = din("w_ada", [D, 6 * D])
    bada_d = din("b_ada", [6 * D])
    win_d = din("w_in", [D, 2048])
    wout_d = din("w_out", [D, D])
    wup_d = din("w_up", [D, 2 * DFF])
    wdn_d = din("w_down", [DFF, D])
    wglu_d = din("w_glu", [512, 512])
    lng_d = din("ln_gb", [4, D])
    convw_d = din("convw", [P, 3, 44])
    convb_d = din("convb", [P, 44])
    bglu_d = din("bglu", [P, 4])
    subg_d = din("subg", [P])
    lamv_d = din("lamv", [4 * 64])
    relb_d = din("relb", [128])
    ar1_d = din("ar1", [P, 4, 64])
    ai1_d = din("ai1", [P, 4, 64])
    ldt1_d = din("ldt1", [P, 4])
    br1_d = din("br1", [P, 4, 64])
    bi1_d = din("bi1", [P, 4, 64])
    d1_d = din("d1", [P, 4])
    ct_d = din("ct", [P, 4, P])
    ar2_d = din("ar2", [P, 16])
    ai2_d = din("ai2", [P, 16])
    ldt2_d = din("ldt2", [P, 16])
    c2r_d = din("c2r", [P, 16, 16])
    c2i_d = din("c2i", [P, 16, 16])
    ident_d = din("ident", [P, P])
    mpar_d = din("mpar", [P, 2])
    mblk_d = din("mblk", [P, 8, 16])
    out_d = nc.dram_tensor("out", [NSEQ, L, D], F32, kind="ExternalOutput").ap()

    winb = dscr("winb", [D, 2048], BF16)
    woutb = dscr("woutb", [D, D], BF16)
    wupb2 = dscr("wupb2", [NJ, P, NK, 256], BF16)
    wdnb = dscr("wdnb", [DFF, D], BF16)
    wglub = dscr("wglub", [512, 512], BF16)
    wadab = dscr("wadab", [D, 6 * D], BF16)
    mods = dscr("mods", [NSEQ, 6, P, D], F32)
    wk_s = dscr("wk_s", [P, 4 * 16 * 128], BF16)
    wgr_s = dscr("wgr_s", [P, 4 * 16 * 128], BF16)
    wgi_s = dscr("wgi_s", [P, 4 * 16 * 128], BF16)
    wyr_s = dscr("wyr_s", [P, 16 * 16 * 32], BF16)
    wyi_s = dscr("wyi_s", [P, 16 * 16 * 32], BF16)
    cat_s = dscr("cat_s", [NSEQ, 8, P, L], BF16)
    dbias_s = dscr("dbias_s", [P, 4 * 2 * P], F32)

    def dbg_dump(name, ap_sb, shape, res, dt=F32):
        if not dbg:
            return
        t = nc.dram_tensor("dbg_" + name, list(shape), dt, kind="ExternalOutput").ap()
        dbg_outs[name] = t
        fw.dma("sp", t, ap_sb, reads=[res], writes=["dbg_" + name])

    V = lambda f: (lambda h: f(h))

    def vts(out, in0, s1, s2, op0, op1=None, reads=(), writes=(), eng="dve"):
        if op1 is None:
            fw.op(eng, lambda h: h.tensor_scalar(out=out, in0=in0, scalar1=s1, scalar2=None, op0=op0),
                  reads, writes)
        else:
            fw.op(eng, lambda h: h.tensor_scalar(out=out, in0=in0, scalar1=s1, scalar2=s2, op0=op0, op1=op1),
                  reads, writes)

    def vtt(out, in0, in1, op, reads=(), writes=(), eng="dve"):
        fw.op(eng, lambda h: h.tensor_tensor(out=out, in0=in0, in1=in1, op=op), reads, writes)

    def vstt(out, in0, s, in1, op0, op1, reads=(), writes=()):
        fw.op("dve", lambda h: h.scalar_tensor_tensor(out=out, in0=in0, scalar=s, in1=in1, op0=op0, op1=op1),
              reads, writes)

    def vcopy(out, in_, reads=(), writes=(), eng="dve"):
        fw.op(eng, lambda h: h.tensor_copy(out=out, in_=in_), reads, writes)

    def act(out, in_, func, bias=None, scale=None, reads=(), writes=()):
        kw = {}
        if bias is not None:
            kw["bias"] = bias
        if scale is not None:
            kw["scale"] = scale
        fw.op("act", lambda h: h.activation(out=out, in_=in_, func=func, **kw), reads, writes)

    def mm(out, lhsT, rhs, start, stop, reads=(), writes=(), inc=True, tp=None):
        if tp is None:
            fw.op("pe", lambda h: h.matmul(out, lhsT=lhsT, rhs=rhs, start=start, stop=stop,
                                           skip_group_check=True), reads, writes, inc=inc)
        else:
            fw.op("pe", lambda h: h.matmul(out, lhsT=lhsT, rhs=rhs, start=start, stop=stop,
                                           skip_group_check=True, tile_position=tp), reads, writes, inc=inc)

    with ExitStack() as G:
        _uid = [0]

        def sb(name, shape, dt, stack=G):
            _uid[0] += 1
            return stack.enter_context(nc.sbuf_tensor("s%d_%s" % (_uid[0], name), list(shape), dt))

        ps = [G.enter_context(nc.psum_tensor("ps%d" % i, [P, 512], F32)) for i in range(8)]
        PS = ["ps%d" % i for i in range(8)]

        identb = sb("identb", [P, P], BF16)
        identf = sb("identf", [P, P], F32)
        g08 = sb("g08", [P, P], F32)
        lamneg = sb("lamneg", [P, 1], F32)
        cw = sb("cw", [P, 3, 44], F32)
        cb = sb("cb", [P, 44], F32)
        bglu = sb("bglu", [P, 4], F32)
        apr = sb("apr", [P, 16, 8], F32)
        api = sb("api", [P, 16, 8], F32)
        napi = sb("napi", [P, 16, 8], F32)

        fw.dma("sp", identf[:], ident_d, writes=["identf"])
        fw.dma("pool", identb[:], ident_d, writes=["identb"])
        fw.dma("sp", cw[:], convw_d, writes=["cw"])
        fw.dma("sp", cb[:], convb_d, writes=["cb"])
        fw.dma("sp", bglu[:], bglu_d, writes=["bglu"])

        mhalf = sb("mhalf", [P, 16], F32)
        SP0 = ExitStack()
        SP0.__enter__()
        lq = sb("lq", [P, 4, 64], F32, SP0)
        rb = sb("rb", [P, 32, 4], F32, SP0)
        sh1 = [P, 4, 64]
        ar1 = sb("ar1", sh1, F32, SP0)
        ai1 = sb("ai1", sh1, F32, SP0)
        br1 = sb("br1", sh1, F32, SP0)
        bi1 = sb("bi1", sh1, F32, SP0)
        ldt1 = sb("ldt1", [P, 4], F32, SP0)
        d1 = sb("d1", [P, 4], F32, SP0)
        ct = sb("ct", [P, 4, P], F32, SP0)
        mpar = sb("mpar", [P, 2], F32, SP0)
        mblk = sb("mblk", [P, 8, 16], F32, SP0)
        sh2 = [P, 16]
        ar2 = sb("ar2", sh2, F32, SP0)
        ai2 = sb("ai2", sh2, F32, SP0)
        ldt2 = sb("ldt2", sh2, F32, SP0)
        c2r = sb("c2r", [P, 16, 16], F32, SP0)
        c2i = sb("c2i", [P, 16, 16], F32, SP0)
        dist = sb("dist", [P, 2, P], F32, SP0)
        fw.op("pool", lambda h: h.iota(dist[:, 0, :], pattern=[[1, P]], base=0, channel_multiplier=-1,
                                       allow_small_or_imprecise_dtypes=True), writes=["dist"])
        fw.op("pool", lambda h: h.iota(dist[:, 1, :], pattern=[[1, P]], base=128, channel_multiplier=-1,
                                       allow_small_or_imprecise_dtypes=True), writes=["dist"])
        fw.op("pool", lambda h: h.memset(mhalf[:], -0.5), writes=["mhalf"])
        fw.dma("sp", lq[:].rearrange("p a b -> p (a b)"), lamv_d.partition_broadcast(P), writes=["lq"])
        fw.dma("sp", g08[:], subg_d.partition_broadcast(P), writes=["g08"])
        fw.dma("sp", rb[:].rearrange("p a b -> p (a b)"), relb_d.partition_broadcast(P), writes=["rb"])
        for t_, d_ in [(ar1, ar1_d), (ai1, ai1_d), (br1, br1_d), (bi1, bi1_d), (ldt1, ldt1_d), (d1, d1_d),
                       (ct, ct_d), (mpar, mpar_d), (mblk, mblk_d)]:
            fw.dma("sp", t_[:], d_, writes=["l1in"])
            fw.lastw.pop("l1in_prev", None)
        for t_, d_ in [(ar2, ar2_d), (ai2, ai2_d), (ldt2, ldt2_d), (c2r, c2r_d), (c2i, c2i_d)]:
            fw.dma("sp", t_[:], d_, writes=["l2in"])
        def cast_w(src, dst, rows, cols, name):
            tot = rows * cols
            per = tot // P
            s2 = src.rearrange("(p r) n -> p (r n)", p=P)
            d2 = dst.rearrange("(p r) n -> p (r n)", p=P)
            step = 8192
            for o in range(0, per, step):
                w = min(step, per - o)
                fw.dma("pool", d2[:, o:o + w], s2[:, o:o + w], reads=["lq", "rb", "l1in", "l2in"],
                       writes=[name])

        cast_w(wada_d, wadab, D, 6 * D, "wadab")
        cast_w(win_d, winb, D, 2048, "winb")
        cast_w(wout_d, woutb, D, D, "woutb")
        cast_w(wglu_d, wglub, 512, 512, "wglub")
        for j in range(NJ):
            for vg in range(2):
                c0 = vg * DFF + 128 * j
                fw.dma("pool", wupb2[j, :, :, 128 * vg:128 * vg + 128],
                       wup_d[:, c0:c0 + 128].rearrange("(k p) n -> p k n", p=P),
                       reads=["lq", "rb", "l1in", "l2in"], writes=["wupb"])
        cast_w(wdn_d, wdnb, DFF, D, "wdnb")

        with ExitStack() as S:
            dbias = sb("dbias", [P, 4, 2, P], F32, S)
            lt = sb("lt", [P, 2, 64], F32, S)
            ls = sb("ls", [P, 2], F32, S)
            le = sb("le", [P, 2], F32, S)
            dl = sb("dl", [P, 32, 4], F32, S)
            acc = sb("acc", [P, P], F32, S)
            tmp = sb("tmpb", [P, P], F32, S)
            vtt(lt[:, 0, :], lq[:, 0, :], lq[:, 1, :], ALU.mult, reads=["lq"], writes=["lt"])
            vtt(lt[:, 1, :], lq[:, 2, :], lq[:, 3, :], ALU.mult, reads=["lq"], writes=["lt"])
            fw.op("dve", lambda h: h.tensor_reduce(out=ls[:], in_=lt[:], axis=AX.X, op=ALU.add),
                  reads=["lt"], writes=["ls"])
            act(le[:], ls[:], AF.Exp, reads=["ls"], writes=["le"])
            vtt(lamneg[:], le[:, 1:2], le[:, 0:1], ALU.subtract, reads=["le"], writes=["lamneg"])
            vts(lamneg[:], lamneg[:], -LAM_INIT, None, ALU.add, reads=["lamneg"], writes=["lamneg"])
            vts(g08[:], g08[:], 1.0 - LAM_INIT, None, ALU.mult, reads=["g08"], writes=["g08"])
            vtt(dl[:, 1:32, :], rb[:, 1:32, :], rb[:, 0:31, :], ALU.subtract, reads=["rb"], writes=["dl"])
            vtt(dl[:, 0, :], rb[:, 0, :], rb[:, 31, :], ALU.subtract, reads=["rb"], writes=["dl"])
            thr = _t5_thresholds()
            for hh in range(4):
                for dd in range(2):
                    vts(acc[:], dist[:, dd, :], 0.0, dl[:, 0, hh:hh + 1], ALU.mult, ALU.add,
                        reads=["dist", "dl"], writes=["acc"])
                    for b in range(1, 32):
                        vts(tmp[:], dist[:, dd, :], float(thr[b - 1]) - 0.5, dl[:, b, hh:hh + 1],
                            ALU.is_ge, ALU.mult, reads=["dist", "dl"], writes=["tmpb"])
                        vtt(acc[:], acc[:], tmp[:], ALU.add, reads=["acc", "tmpb"], writes=["acc"])
                    if dd == 0:
                        vts(tmp[:], dist[:, 0, :], -0.5, -1e30, ALU.is_lt, ALU.mult, reads=["dist"], writes=["tmpb"])
                        vstt(dbias[:, hh, 0, :], acc[:], 8.0, tmp[:], ALU.mult, ALU.add,
                             reads=["acc", "tmpb"], writes=["dbias"])
                    else:
                        vts(dbias[:, hh, 1, :], acc[:], 8.0, None, ALU.mult, reads=["acc"], writes=["dbias"])
            fw.dma("sp", dbias_s, dbias[:].rearrange("p a b c -> p (a b c)"), reads=["dbias"], writes=["dbias_s"])
            fw.barrier()

        def sincos(S, name, ang, shape, res):
            outs = []
            for idx, shift in enumerate((0.0, math.pi / 2)):
                a = sb("%s_a%d" % (name, idx), shape, F32, S)
                ki = sb("%s_k%d" % (name, idx), shape, I32, S)
                kf = sb("%s_f%d" % (name, idx), shape, F32, S)
                r = sb("%s_r%d" % (name, idx), shape, F32, S)
                t = sb("%s_t%d" % (name, idx), shape, F32, S)
                o = sb("%s_o%d" % (name, idx), shape, F32, S)
                rn = "%s_r%d" % (name, idx)
                vts(a[:], ang, shift, None, ALU.add, reads=[res], writes=[rn + "a"])
                vts(t[:], a[:], 1.0 / TWO_PI, None, ALU.mult, reads=[rn + "a"], writes=[rn + "t"])
                vcopy(ki[:], t[:], reads=[rn + "t"], writes=[rn + "k"])
                vcopy(kf[:], ki[:], reads=[rn + "k"], writes=[rn + "f"])
                vstt(r[:], kf[:], -TWO_PI, a[:], ALU.mult, ALU.add, reads=[rn + "f", rn + "a"], writes=[rn])
                vts(t[:], r[:], math.pi, -TWO_PI, ALU.is_gt, ALU.mult, reads=[rn], writes=[rn + "t"])
                vtt(r[:], r[:], t[:], ALU.add, reads=[rn, rn + "t"], writes=[rn])
                vts(t[:], r[:], -math.pi, TWO_PI, ALU.is_lt, ALU.mult, reads=[rn], writes=[rn + "t"])
                vtt(r[:], r[:], t[:], ALU.add, reads=[rn, rn + "t"], writes=[rn])
                act(o[:], r[:], AF.Sin, reads=[rn], writes=[rn + "o"])
                outs.append((o, rn + "o"))
            return outs

        def abar(S, name, ar, ai, ldt_bc, shape, res):
            dtt = sb(name + "_dt", shape, F32, S)
            xx = sb(name + "_x", shape, F32, S)
            mag = sb(name + "_mag", shape, F32, S)
            ang = sb(name + "_ang", shape, F32, S)
            abr = sb(name + "_abr", shape, F32, S)
            abi = sb(name + "_abi", shape, F32, S)
            act(dtt[:], ldt_bc, AF.Exp, reads=[res], writes=[name + "dt"])
            vtt(xx[:], ar, dtt[:], ALU.mult, reads=[res, name + "dt"], writes=[name + "x"])
            act(mag[:], xx[:], AF.Exp, reads=[name + "x"], writes=[name + "mag"])
            vtt(ang[:], ai, dtt[:], ALU.mult, reads=[res, name + "dt"], writes=[name + "ang"])
            (sn, snr), (cs, csr) = sincos(S, name + "sc", ang[:], shape, name + "ang")
            vtt(abr[:], mag[:], cs[:], ALU.mult, reads=[name + "mag", csr], writes=[name + "abr"])
            vtt(abi[:], mag[:], sn[:], ALU.mult, reads=[name + "mag", snr], writes=[name + "abi"])
            return abr, abi

        with ExitStack() as S:
            sh1 = [P, 4, 64]
            vts(ct[64:128], ct[64:128], -1.0, None, ALU.mult, reads=["l1in"], writes=["ct"])
            shz = [P, 4, 16, 64]
            zc = sb("zc", [P, 4, 16, P], F32, S)
            zr = zc[:, :, :, 0:64]
            zi = zc[:, :, :, 64:128]
            zt = sb("zt", shz, F32, S)
            with ExitStack() as SA:
                abr, abi = abar(SA, "a1", ar1[:], ai1[:], ldt1[:].unsqueeze(2).to_broadcast(sh1), sh1, "l1in")
                RA = ["a1abr", "a1abi", "l1in"]
                den = sb("den", sh1, F32, SA)
                t1 = sb("t1s", sh1, F32, SA)
                t2 = sb("t2s", sh1, F32, SA)
                fr = sb("fr", sh1, F32, SA)
                fi = sb("fi", sh1, F32, SA)
                bbr = sb("bbr", sh1, F32, SA)
                bbi = sb("bbi", sh1, F32, SA)
                am1 = sb("am1", sh1, F32, SA)
                vtt(den[:], ar1[:], ar1[:], ALU.mult, reads=RA, writes=["den"])
                vtt(t1[:], ai1[:], ai1[:], ALU.mult, reads=RA, writes=["t1s"])
                vtt(den[:], den[:], t1[:], ALU.add, reads=["den", "t1s"], writes=["den"])
                fw.op("dve", lambda h: h.reciprocal(out=den[:], in_=den[:]), reads=["den"], writes=["den"])
                vts(am1[:], abr[:], -1.0, None, ALU.add, reads=RA, writes=["am1"])
                vtt(t1[:], am1[:], ar1[:], ALU.mult, reads=["am1", "l1in", "den"], writes=["t1s"])
                vtt(t2[:], abi[:], ai1[:], ALU.mult, reads=RA, writes=["t2s"])
                vtt(t1[:], t1[:], t2[:], ALU.add, reads=["t1s", "t2s"], writes=["t1s"])
                vtt(fr[:], t1[:], den[:], ALU.mult, reads=["t1s", "den"], writes=["fr"])
                vtt(t1[:], abi[:], ar1[:], ALU.mult, reads=RA + ["fr"], writes=["t1s"])
                vtt(t2[:], am1[:], ai1[:], ALU.mult, reads=["am1", "l1in", "fr"], writes=["t2s"])
                vtt(t1[:], t1[:], t2[:], ALU.subtract, reads=["t1s", "t2s"], writes=["t1s"])
                vtt(fi[:], t1[:], den[:], ALU.mult, reads=["t1s", "den"], writes=["fi"])
                vtt(t1[:], fr[:], br1[:], ALU.mult, reads=["fr", "l1in", "fi"], writes=["t1s"])
                vtt(t2[:], fi[:], bi1[:], ALU.mult, reads=["fi", "l1in"], writes=["t2s"])
                vtt(bbr[:], t1[:], t2[:], ALU.subtract, reads=["t1s", "t2s"], writes=["bbr"])
                vtt(t1[:], fr[:], bi1[:], ALU.mult, reads=["fr", "l1in", "bbr"], writes=["t1s"])
                vtt(t2[:], fi[:], br1[:], ALU.mult, reads=["fi", "l1in", "bbr"], writes=["t2s"])
                vtt(bbi[:], t1[:], t2[:], ALU.add, reads=["t1s", "t2s"], writes=["bbi"])
                pwr = sb("pwr", [P, 4, 17, 64], F32, SA)
                pwi = sb("pwi", [P, 4, 17, 64], F32, SA)
                fw.op("dve", lambda h: h.memset(pwr[:, :, 0, :], 1.0), writes=["pw"])
                fw.op("dve", lambda h: h.memset(pwi[:, :, 0, :], 0.0), writes=["pw"])
                for j in range(16):
                    vtt(t1[:], pwr[:, :, j, :], abr[:], ALU.mult, reads=["pw"] + RA, writes=["t1s"])
                    vtt(t2[:], pwi[:, :, j, :], abi[:], ALU.mult, reads=["pw"] + RA, writes=["t2s"])
                    vtt(pwr[:, :, j + 1, :], t1[:], t2[:], ALU.subtract, reads=["t1s", "t2s"], writes=["pw"])
                    vtt(t1[:], pwr[:, :, j, :], abi[:], ALU.mult, reads=["pw"] + RA, writes=["t1s"])
                    vtt(t2[:], pwi[:, :, j, :], abr[:], ALU.mult, reads=["pw"] + RA, writes=["t2s"])
                    vtt(pwi[:, :, j + 1, :], t1[:], t2[:], ALU.add, reads=["t1s", "t2s"], writes=["pw"])
                bbr_b = bbr[:].unsqueeze(2).to_broadcast(shz)
                bbi_b = bbi[:].unsqueeze(2).to_broadcast(shz)
                vtt(zr[:], pwr[:, :, 0:16, :], bbr_b, ALU.mult, reads=["pw", "bbr"], writes=["zr"])
                vtt(zt[:], pwi[:, :, 0:16, :], bbi_b, ALU.mult, reads=["pw", "bbi"], writes=["zt"])
                vtt(zr[:], zr[:], zt[:], ALU.subtract, reads=["zr", "zt"], writes=["zr"])
                vtt(zi[:], pwr[:, :, 0:16, :], bbi_b, ALU.mult, reads=["pw", "bbi", "zr"], writes=["zi"])
                vtt(zt[:], pwi[:, :, 0:16, :], bbr_b, ALU.mult, reads=["pw", "bbr", "zr"], writes=["zt"])
                vtt(zi[:], zi[:], zt[:], ALU.add, reads=["zi", "zt"], writes=["zi"])
                fw.barrier()
            with ExitStack() as SB:
                wg_r = sb("wg_r", [P, 4, 16, P], BF16, SB)
                wg_i = sb("wg_i", [P, 4, 16, P], BF16, SB)
                for tau in range(16):
                    for (wg, zz, rn) in ((wg_r, zr, "zr"), (wg_i, zi, "zi")):
                        for par in range(2):
                            vts(wg[:, :, tau, 64 * par:64 * par + 64], zz[:, :, 15 - tau, :], mpar[:, par:par + 1], None,
                                ALU.mult, reads=[rn, "l1in"], writes=["wg"])
                fw.dma("sp", wgr_s, wg_r[:].rearrange("p a b c -> p (a b c)"), reads=["wg"], writes=["wgs"])
                fw.dma("sp", wgi_s, wg_i[:].rearrange("p a b c -> p (a b c)"), reads=["wg"], writes=["wgs"])
                fw.barrier()
            with ExitStack() as SC:
                wk = sb("wk", [P, 4, 16, P], BF16, SC)
                zT = [sb("zT%d" % i, [P, P], F32, SC) for i in range(2)]
                ktmp = sb("ktmp", [P, P], F32, SC)
                m128 = mblk[:].rearrange("p g h -> p (g h)")
                it = 0
                for t in range(4):
                    for j in range(16):
                        b_ = it % 2
                        it += 1
                        fw.op("pe", lambda h: h.transpose(ps[b_][:, 0:P], zc[:, t, j, :], identf[:]),
                              reads=["zr", "zi", "identf"], writes=[PS[b_]])
                        act(zT[b_][:], ps[b_][:, 0:P], AF.Copy, writes=[PS[b_], "zT%d" % b_])
                        mm(ps[2 + b_][:, 0:P], zT[b_][:], ct[:, t, :], True, True, reads=["zT%d" % b_, "ct"],
                           writes=[PS[2 + b_]])
                        if j == 0:
                            vtt(ktmp[:], ps[2 + b_][:, 0:P], m128, ALU.mult, reads=["l1in"],
                                writes=[PS[2 + b_], "ktmp"])
                            vstt(wk[:, t, j, :], identf[:], d1[:, t:t + 1], ktmp[:], ALU.mult, ALU.add,
                                 reads=["identf", "l1in", "ktmp"], writes=["wk"])
                        else:
                            vtt(wk[:, t, j, :], ps[2 + b_][:, 0:P], m128, ALU.mult, reads=["l1in"],
                                writes=[PS[2 + b_], "wk"])
                fw.dma("sp", wk_s, wk[:].rearrange("p a b c -> p (a b c)"), reads=["wk"], writes=["wks"])
                fw.barrier()
        with ExitStack() as S:
            sh2 = [P, 16]
            abr, abi = abar(S, "a2", ar2[:], ai2[:], ldt2[:], sh2, "l2in")
            RA = ["a2abr", "a2abi"]
            p2r = sb("p2r", [P, 16, 17], F32, S)
            p2i = sb("p2i", [P, 16, 17], F32, S)
            u1 = sb("u1", sh2, F32, S)
            u2 = sb("u2", sh2, F32, S)
            fw.op("dve", lambda h: h.memset(p2r[:, :, 0], 1.0), writes=["p2"])
            fw.op("dve", lambda h: h.memset(p2i[:, :, 0], 0.0), writes=["p2"])
            for j in range(16):
                vtt(u1[:], p2r[:, :, j], abr[:], ALU.mult, reads=["p2"] + RA, writes=["u1"])
                vtt(u2[:], p2i[:, :, j], abi[:], ALU.mult, reads=["p2"] + RA, writes=["u2"])
                vtt(p2r[:, :, j + 1], u1[:], u2[:], ALU.subtract, reads=["u1", "u2"], writes=["p2"])
                vtt(u1[:], p2r[:, :, j], abi[:], ALU.mult, reads=["p2"] + RA, writes=["u1"])
                vtt(u2[:], p2i[:, :, j], abr[:], ALU.mult, reads=["p2"] + RA, writes=["u2"])
                vtt(p2i[:, :, j + 1], u1[:], u2[:], ALU.add, reads=["u1", "u2"], writes=["p2"])
            vcopy(apr[:, :, 0], p2r[:, :, 16], reads=["p2"], writes=["ap"])
            vcopy(api[:, :, 0], p2i[:, :, 16], reads=["p2"], writes=["ap"])
            for r in range(7):
                vtt(u1[:], apr[:, :, r], apr[:, :, r], ALU.mult, reads=["ap"], writes=["u1"])
                vtt(u2[:], api[:, :, r], api[:, :, r], ALU.mult, reads=["ap"], writes=["u2"])
                vtt(apr[:, :, r + 1], u1[:], u2[:], ALU.subtract, reads=["u1", "u2"], writes=["ap"])
                vtt(u1[:], apr[:, :, r], api[:, :, r], ALU.mult, reads=["ap"], writes=["u1"])
                vts(api[:, :, r + 1], u1[:], 2.0, None, ALU.mult, reads=["u1"], writes=["ap"])
            vts(napi[:], api[:], -1.0, None, ALU.mult, reads=["ap"], writes=["ap"])
            wy_r = sb("wy_r", [P, 16, 16, 32], BF16, S)
            wy_i = sb("wy_i", [P, 16, 16, 32], BF16, S)
            w1 = sb("w1", [P, 16, 16, 16], F32, S)
            w2 = sb("w2", [P, 16, 16, 16], F32, S)
            fw.op("dve", lambda h: h.memset(wy_r[:], 0.0), writes=["wy"])
            fw.op("dve", lambda h: h.memset(wy_i[:], 0.0), writes=["wy"])
            shw = [P, 16, 16, 16]
            crb = c2r[:].unsqueeze(2).to_broadcast(shw)
            cib = c2i[:].unsqueeze(2).to_broadcast(shw)
            prb = p2r[:, :, 1:17].unsqueeze(3).to_broadcast(shw)
            pib = p2i[:, :, 1:17].unsqueeze(3).to_broadcast(shw)
            vtt(w1[:], crb, prb, ALU.mult, reads=["l2in", "p2"], writes=["w1"])
            vtt(w2[:], cib, pib, ALU.mult, reads=["l2in", "p2"], writes=["w2"])
            for e in range(2):
                vtt(wy_r[64 * e:64 * e + 64, :, :, 16 * e:16 * e + 16], w1[64 * e:64 * e + 64],
                    w2[64 * e:64 * e + 64], ALU.subtract, reads=["w1", "w2"], writes=["wy"])
            vtt(w1[:], crb, pib, ALU.mult, reads=["l2in", "p2", "wy"], writes=["w1"])
            vtt(w2[:], cib, prb, ALU.mult, reads=["l2in", "p2", "wy"], writes=["w2"])
            vtt(w1[:], w1[:], w2[:], ALU.add, reads=["w1", "w2"], writes=["w1"])
            for e in range(2):
                vts(wy_i[64 * e:64 * e + 64, :, :, 16 * e:16 * e + 16], w1[64 * e:64 * e + 64], -1.0, None,
                    ALU.mult, reads=["w1"], writes=["wy"])
            fw.dma("sp", wyr_s, wy_r[:].rearrange("p a b c -> p (a b c)"), reads=["wy"], writes=["wys"])
            fw.dma("sp", wyi_s, wy_i[:].rearrange("p a b c -> p (a b c)"), reads=["wy"], writes=["wys"])
            fw.barrier()

        SP0.__exit__(None, None, None)
        with ExitStack() as S:
            csb = sb("csb", [P, NSEQ, NK], F32, S)
            ca = sb("ca", [P, NSEQ, NK], F32, S)
            cbb = sb("cbb", [P, NSEQ, NK, P], BF16, S)
            ba = sb("ba", [P, 6 * D], F32, S)
            wa = [sb("wa%d" % i, [P, NK, 512], BF16, S) for i in range(2)]
            modt = [sb("modt%d" % i, [P, 512], F32, S) for i in range(2)]
            fw.dma("sp", csb[:], cT_d, writes=["csb"])
            fw.dma("sp", ba[:], bada_d.partition_broadcast(P), writes=["ba"])
            act(ca[:], csb[:], AF.Silu, reads=["csb"], writes=["ca"])
            for b in range(NSEQ):
                vcopy(cbb[:, b, :, :], ca[:, b, :].unsqueeze(2).to_broadcast([P, NK, P]),
                      reads=["ca"], writes=["cbb"])
            wv = wadab.rearrange("(k p) n -> p k n", p=P)
            it = 0
            for n in range(12):
                wt = wa[n % 2]
                fw.dma("sp", wt[:], wv[:, :, 512 * n:512 * n + 512], reads=["wadab"], writes=["wa%d" % (n % 2)])
                for b in range(NSEQ):
                    bank = it % 2
                    for k in range(NK):
                        mm(ps[bank][:], cbb[:, b, k, :], wt[:, k, :], k == 0, k == NK - 1,
                           reads=["cbb", "wa%d" % (n % 2)], writes=[PS[bank]], inc=(k == NK - 1))
                    mt = modt[it % 2]
                    one = 1.0 if (n // 2) in (1, 4) else 0.0
                    vstt(mt[:], ps[bank][:], one, ba[:, 512 * n:512 * n + 512], ALU.add, ALU.add,
                         reads=["ba"], writes=[PS[bank], "modt%d" % (it % 2)])
                    fw.dma("sp", mods[b, n // 2, :, 512 * (n % 2):512 * (n % 2) + 512], mt[:],
                           reads=["modt%d" % (it % 2)], writes=["mods"])
                    it += 1
            fw.barrier()

        def ln_stats(S_, xt_ap, xres, tag, st, mv, sd, rstd):
            for hlf in range(2):
                fw.op("dve", lambda h: h.bn_stats(out=st[:, 6 * hlf:6 * hlf + 6],
                                                  in_=xt_ap[:, 512 * hlf:512 * hlf + 512]),
                      reads=[xres], writes=[tag + "st"])
            fw.op("dve", lambda h: h.bn_aggr(out=mv[:], in_=st[:]), reads=[tag + "st"], writes=[tag + "mv"])
            vts(sd[:], mv[:, 1:2], EPS, None, ALU.add, reads=[tag + "mv"], writes=[tag + "sd"])
            vtt(rstd[:], sd[:], mhalf[:, 0:1], ALU.pow, reads=[tag + "sd", "mhalf"], writes=[tag + "rstd"], eng="pool")

        epsb = sb("epsb", [P, 1], F32)
        fw.op("dve", lambda h: h.memset(epsb[:], EPS), writes=["epsb"])

        for s in range(NSEQ):
            with ExitStack() as SQ:
                std = sb("std", [P, 4, TT, NC16], BF16, SQ)
                with ExitStack() as S:
                    ktb = sb("ktb", [P, 4, L], BF16, S)
                    vb = sb("vb", [P, NB, 4, 129], BF16, S)
                    wi = sb("wi", [P, NK, 2048], BF16, S)
                    a1 = sb("a1", [P, D], F32, S)
                    s1 = sb("s1", [P, D], F32, S)
                    xt = [sb("xt%d" % i, [P, D], F32, S) for i in range(2)]
                    ub = sb("ub", [P, D], BF16, S)
                    uts = [sb("ut%d" % i, [P, NK, 512], BF16, S) for i in range(2)]
                    qt = sb("qt", [P, 4, 512], BF16, S)
                    pt = [[sb("pt%d%d" % (m, i), [P, 512], BF16, S) for i in range(2)] for m in range(2)]
                    ost = [sb("ost%d" % i, [P, 4, 512], BF16, S) for i in range(2)]
                    st = sb("st", [P, 12], F32, S)
                    mv = sb("mv", [P, 2], F32, S)
                    sd = sb("sd", [P, 1], F32, S)
                    rstd = sb("rstd", [P, 1], F32, S)
                    osq = sb("osq", [P, P], F32, S)
                    rz = sb("rz", [P, 2], F32, S)
                    oraw = [sb("oraw%d" % i, [P, 258], F32, S) for i in range(3)]
                    osb_all = sb("osb_all", [P, 16, P], F32, S)
                    onb_all = sb("onb_all", [P, 16, P], BF16, S)
                    ss_all = sb("ss_all", [P, 16], F32, S)
                    sd16 = sb("sd16", [P, 16], F32, S)
                    rs16 = sb("rs16", [P, 16], F32, S)
                    fin_n = [0]
                    pending = []
                    dbias = sb("dbias", [P, 4, 2, P], F32, S)
                    fw.dma("sp", dbias[:].rearrange("p a b c -> p (a b c)"), dbias_s, reads=["dbias_s"], writes=["dbias"])
                    fw.dma("sp", wi[:], winb.rearrange("(k p) n -> p k n", p=P), reads=["winb"], writes=["wi"])
                    fw.dma("sp", a1[:], mods[s, 1], reads=["mods"], writes=["a1"])
                    fw.dma("sp", s1[:], mods[s, 0], reads=["mods"], writes=["s1"])
                    fw.op("pool", lambda h: h.memset(vb[:, :, :, 128:129], 1.0), writes=["vb"])
                    def lnf_tile(c, i):
                        xx = xt[i % 2]
                        xr = "xt%d" % (i % 2)
                        tok0 = 512 * c + 128 * i
                        fw.dma("sp", xx[:], x_d[s, tok0:tok0 + 128, :], writes=[xr])
                        ln_stats(S, xx, xr, "a", st, mv, sd, rstd)
                        vstt(xx[:], xx[:], mv[:, 0:1], a1[:], ALU.subtract, ALU.mult,
                             reads=[xr, "amv", "a1"], writes=[xr])
                        vstt(ub[:], xx[:], rstd[:, 0:1], s1[:], ALU.mult, ALU.add,
                             reads=[xr, "arstd", "s1"], writes=["ub"])

                    def lnf_T(c, i):
                        pst = ps[3][:].bitcast(BF16)
                        for k in range(NK):
                            fw.op("pe", lambda h: h.transpose(pst[:, 128 * k:128 * k + 128],
                                                              ub[:, 128 * k:128 * k + 128], identb[:]),
                                  reads=["ub", "identb"], writes=[PS[3]], inc=(k == NK - 1))

                    def lnf_copy(c, i):
                        pst = ps[3][:].bitcast(BF16)
                        act(uts[c % 2][:, :, 128 * i:128 * i + 128], pst.rearrange("p (k t) -> p k t", k=NK), AF.Copy,
                            writes=[PS[3], "ut%d" % (c % 2)])

                    for ci in range(NCH):
                        ut = uts[ci % 2]
                        utn = "ut%d" % (ci % 2)
                        if ci == 0:
                            for i in range(4):
                                lnf_tile(0, i)
                                lnf_T(0, i)
                                lnf_copy(0, i)
                        bi_ = 0
                        for m in range(12):
                            bank = 1 + (bi_ % 3)
                            bi_ += 1
                            col0 = 128 * m if m < 8 else 1536 + 128 * (m - 8)
                            for k in range(NK):
                                mm(ps[bank][:], wi[:, k, col0:col0 + 128], ut[:, k, :], k == 0, k == NK - 1,
                                   reads=["wi", utn], writes=[PS[bank]], inc=(k == NK - 1))
                            if m < 4:
                                act(qt[:, m, :], ps[bank][:], AF.Copy, writes=[PS[bank], "qt"])
                            elif m < 8:
                                vcopy(ktb[:, m - 4, 512 * ci:512 * ci + 512], ps[bank][:],
                                      writes=[PS[bank], "ktb"])
                            else:
                                act(std[:, m - 8, :, 32 * ci:32 * ci + 32],
                                    ps[bank][:].rearrange("p (c t) -> p t c", t=TT), AF.Copy,
                                    writes=[PS[bank], "std"])
                        for i in range(4):
                            bank = 1 + (bi_ % 3)
                            bi_ += 1
                            for k in range(NK):
                                mm(ps[bank][:], ut[:, k, 128 * i:128 * i + 128], wi[:, k, 1024:1536], k == 0,
                                   k == NK - 1, reads=["wi", utn], writes=[PS[bank]], inc=(k == NK - 1))
                            vcopy(vb[:, 4 * ci + i, :, 0:128], ps[bank][:].rearrange("p (h e) -> p h e", h=4),
                                  writes=[PS[bank], "vb"])
                        while pending:
                            pending.pop()()
                        osx = ost[ci % 2]
                        osr = "ost%d" % (ci % 2)
                        cp_pending = []
                        for hh in range(4):
                            nblk = 4 * ci + 4
                            if ci + 1 < NCH:
                                lnf_tile(ci + 1, hh)

                            def qk(j, m):
                                r = j - 4 * ci
                                q0 = 128 * max(r, 0)
                                bank = (2 * j + m) % 3
                                special = [i for i in range(4) if 0 <= i - r <= 1]
                                mm(ps[bank][:, q0:512], ktb[64 * m:64 * m + 64, hh, 128 * j:128 * j + 128],
                                   qt[64 * m:64 * m + 64, hh, q0:512], True, len(special) == 0,
                                   reads=["ktb", "qt"], writes=[PS[bank]], inc=(len(special) == 0))
                                for n_, i in enumerate(special):
                                    last = n_ == len(special) - 1
                                    mm(ps[bank][:, 128 * i:128 * i + 128], identf[:], dbias[:, hh, i - r, :],
                                       False, last, reads=["identf", "dbias"], writes=[PS[bank]], inc=last)

                            qk(0, 0)
                            qk(0, 1)
                            for j in range(nblk):
                                r = j - 4 * ci
                                q0 = 128 * max(r, 0)
                                if j + 1 < nblk:
                                    qk(j + 1, 0)
                                if j == min(1, nblk - 1) and cp_pending:
                                    lnf_copy(*cp_pending.pop())
                                for m in range(2):
                                    bank = (2 * j + m) % 3
                                    pr = "pt%d%d" % (m, j % 2)
                                    act(pt[m][j % 2][:, q0:512], ps[bank][:, q0:512], AF.Exp, scale=0.125,
                                        writes=[PS[bank], pr])
                                    if m == 0 and j + 1 < nblk:
                                        qk(j + 1, 1)
                                for i in range(max(r, 0), 4):
                                    ob = 4 + i
                                    for m in range(2):
                                        pr = "pt%d%d" % (m, j % 2)
                                        lastj = (j == 4 * ci + i)
                                        mm(ps[ob][:, 129 * m:129 * m + 129], pt[m][j % 2][:, 128 * i:128 * i + 128],
                                           vb[:, j, hh, :], (j == 0 and m == 0), lastj,
                                           reads=[pr, "vb"], writes=[PS[ob]], inc=(lastj and m == 1))
                                    if j == 4 * ci + i:
                                        pso = ps[ob]
                                        orw = oraw[fin_n[0] % 3]
                                        orn = "oraw%d" % (fin_n[0] % 3)
                                        fin_n[0] += 1
                                        idx = 4 * hh + i
                                        vcopy(orw[:], pso[:, 0:258], writes=[PS[ob], orn])
                                        fw.op("dve", lambda h: h.reciprocal(
                                            out=rz[:], in_=orw[:, 128:258:129]), reads=[orn], writes=["rz"])
                                        vtt(rz[:, 1:2], rz[:, 1:2], lamneg[:], ALU.mult, reads=["rz", "lamneg"],
                                            writes=["rz"])
                                        vts(osb_all[:, idx, :], orw[:, 0:128], rz[:, 0:1], None, ALU.mult,
                                            reads=["rz", orn], writes=["osb_all"])
                                        vstt(osb_all[:, idx, :], orw[:, 129:257], rz[:, 1:2], osb_all[:, idx, :],
                                             ALU.mult, ALU.add, reads=["rz", orn, "osb_all"], writes=["osb_all"])
                                        vtt(osq[:], osb_all[:, idx, :], osb_all[:, idx, :], ALU.mult,
                                            reads=["osb_all"], writes=["osq"])
                                        fw.op("dve", lambda h: h.tensor_reduce(out=ss_all[:, idx:idx + 1], in_=osq[:],
                                                                               axis=AX.X, op=ALU.add),
                                              reads=["osq"], writes=["ss_all"])
                            if ci + 1 < NCH:
                                lnf_T(ci + 1, hh)
                                cp_pending.append((ci + 1, hh))
                        while cp_pending:
                            lnf_copy(*cp_pending.pop())
                        vts(sd16[:], ss_all[:], 1.0 / 128.0, EPS, ALU.mult, ALU.add, reads=["ss_all"], writes=["sd16"])
                        vtt(rs16[:], sd16[:], mhalf[:], ALU.pow, reads=["sd16", "mhalf"], writes=["rs16"], eng="pool")
                        for idx in range(16):
                            vstt(onb_all[:, idx, :], osb_all[:, idx, :], rs16[:, idx:idx + 1], g08[:], ALU.mult,
                                 ALU.mult, reads=["osb_all", "rs16", "g08"], writes=["onb_all"])

                        def end_fin(ci=ci, osx=osx, osr=osr):
                            for half in range(2):
                                bank = 4 + half
                                pb = ps[bank][:].bitcast(BF16)
                                for n_ in range(8):
                                    idx = 8 * half + n_
                                    fw.op("pe", lambda h: h.transpose(pb[:, 128 * n_:128 * n_ + 128],
                                                                      onb_all[:, idx, :], identb[:]),
                                          reads=["onb_all", "identb"], writes=[PS[bank]], inc=(n_ == 7))
                                vcopy(osx[:, 2 * half:2 * half + 2, :], pb.rearrange("p (h t) -> p h t", h=2),
                                      writes=[PS[bank], osr])
                            for hh in range(4):
                                fw.dma("sp", cat_s[s, hh, :, 512 * ci:512 * ci + 512], osx[:, hh, :], reads=[osr],
                                       writes=["cat"])
                        pending.append(end_fin)
                        if dbg and s == 0 and ci == 0:
                            pending.pop()()
                            dbg_dump("ub", ub[:], [P, D], "ub", BF16)
                            dbg_dump("qt", qt[:], [P, 4, 512], "qt", BF16)
                            dbg_dump("ost", osx[:], [P, 4, 512], osr, BF16)
                    while pending:
                        pending.pop()()
                    if dbg and s == 0:
                        dbg_dump("ktb", ktb[:], [P, 4, L], "ktb", BF16)
                        dbg_dump("vb", vb[:], [P, NB, 4, 129], "vb", BF16)
                        dbg_dump("std", std[:], [P, 4, TT, NC16], "std", BF16)
                    fw.barrier()

                with ExitStack() as S2:
                    with ExitStack() as S:
                        yb = sb("yb", [P, 4, TT, NC16], BF16, S)
                        with ExitStack() as S3:
                            hbr = sb("hbr", [P, 16, NC16 + 1], BF16, S3)
                            hbi = sb("hbi", [P, 16, NC16 + 1], BF16, S3)
                            wgr = sb("wgr", [P, 4, 16, P], BF16, S3)
                            wgi = sb("wgi", [P, 4, 16, P], BF16, S3)
                            wkk = sb("wkk", [P, 4, 16, P], BF16, S3)
                            wyr = sb("wyr", [P, 16, 16, 32], BF16, S3)
                            wyi = sb("wyi", [P, 16, 16, 32], BF16, S3)
                            fw.dma("sp", wgr[:].rearrange("p a b c -> p (a b c)"), wgr_s, reads=["wgs"], writes=["wgr"])
                            fw.dma("sp", wgi[:].rearrange("p a b c -> p (a b c)"), wgi_s, reads=["wgs"], writes=["wgi"])
                            fw.dma("sp", wkk[:].rearrange("p a b c -> p (a b c)"), wk_s, reads=["wks"], writes=["wkk"])
                            fw.dma("sp", wyr[:].rearrange("p a b c -> p (a b c)"), wyr_s, reads=["wys"], writes=["wyr"])
                            fw.dma("sp", wyi[:].rearrange("p a b c -> p (a b c)"), wyi_s, reads=["wys"], writes=["wyi"])
                            fw.op("pool", lambda h: h.memset(hbr[:, :, 0:1], 0.0), writes=["hbr"])
                            fw.op("pool", lambda h: h.memset(hbi[:, :, 0:1], 0.0), writes=["hbi"])
                            kb = [[sb("kb%d%d" % (a_, b_), [P, NC16], F32, S3) for b_ in range(2)] for a_ in range(4)]
                            yt = [sb("yt%d" % i, [P, 512], F32, S3) for i in range(2)]
                            y2 = [sb("y2%d" % i, [P, 512], F32, S3) for i in range(2)]
                            y3 = [sb("y3%d" % i, [P, 512], F32, S3) for i in range(2)]

                            def g_mm(q, set_):
                                t = q // 4
                                r0 = 32 * (q % 4)
                                hr_a, hi_a = kb[2 * set_][0], kb[2 * set_][1]
                                na = "kba%d" % set_
                                for comp, (wg, dst) in enumerate(((wgr, hr_a), (wgi, hi_a))):
                                    for cb_ in range(0, NC16, 512):
                                        w_ = min(512, NC16 - cb_)
                                        bank = 4 + (2 * q + comp) % 4
                                        for tau in range(16):
                                            mm(ps[bank][:, 0:w_], wg[r0:r0 + 32, t, tau, :],
                                               std[r0:r0 + 32, t, tau, cb_:cb_ + w_], tau == 0, tau == 15,
                                               reads=["wgr", "wgi", "std"], writes=[PS[bank]], inc=(tau == 15),
                                               tp=(r0, 0))
                                        act(dst[:, cb_:cb_ + w_], ps[bank][:, 0:w_], AF.Copy, writes=[PS[bank], na])

                            def ks_pair(qs):
                                st_ = []
                                for set_, q in enumerate(qs):
                                    st_.append([(kb[2 * set_][0], kb[2 * set_][1], "kba%d" % set_),
                                                (kb[2 * set_ + 1][0], kb[2 * set_ + 1][1], "kbb%d" % set_)])
                                d_ = 1
                                rnd = 0
                                while d_ < NC16:
                                    n_ = NC16 - d_
                                    for set_, q in enumerate(qs):
                                        (cr_, ci_, cn), (or_, oi_, on) = st_[set_]
                                        arp = apr[:, q, rnd:rnd + 1]
                                        aip = api[:, q, rnd:rnd + 1]
                                        naip = napi[:, q, rnd:rnd + 1]
                                        vcopy(or_[:, 0:d_], cr_[:, 0:d_], reads=[cn], writes=[on], eng="pool")
                                        vcopy(oi_[:, 0:d_], ci_[:, 0:d_], reads=[cn], writes=[on], eng="pool")
                                        vstt(or_[:, d_:], cr_[:, 0:n_], arp, cr_[:, d_:], ALU.mult, ALU.add,
                                             reads=[cn, "ap"], writes=[on])
                                        vstt(oi_[:, d_:], ci_[:, 0:n_], arp, ci_[:, d_:], ALU.mult, ALU.add,
                                             reads=[cn, "ap"], writes=[on])
                                        vstt(or_[:, d_:], ci_[:, 0:n_], naip, or_[:, d_:], ALU.mult, ALU.add,
                                             reads=[cn, on, "ap"], writes=[on])
                                        vstt(oi_[:, d_:], cr_[:, 0:n_], aip, oi_[:, d_:], ALU.mult, ALU.add,
                                             reads=[cn, on, "ap"], writes=[on])
                                        st_[set_].reverse()
                                    d_ *= 2
                                    rnd += 1
                                for set_, q in enumerate(qs):
                                    cr_, ci_, cn = st_[set_][0]
                                    vcopy(hbr[:, q, 1:NC16 + 1], cr_[:], reads=[cn], writes=["hbr"], eng="pool")
                                    vcopy(hbi[:, q, 1:NC16 + 1], ci_[:], reads=[cn], writes=["hbi"], eng="pool")

                            yit = [0]

                            def y_tile(t):
                                for tt in range(TT):
                                    for cb_ in range(0, NC16, 512):
                                        w_ = min(512, NC16 - cb_)
                                        b_ = yit[0] % 2
                                        yit[0] += 1
                                        bi_a, bi_b = 2 * b_, 2 * b_ + 1
                                        for tau in range(tt + 1):
                                            mm(ps[bi_a][:, 0:w_], wkk[:, t, tt - tau, :],
                                               std[:, t, tau, cb_:cb_ + w_], tau == 0, tau == tt,
                                               reads=["wkk", "std"], writes=[PS[bi_a]], inc=(tau == tt))
                                        for qq in range(4):
                                            q = 4 * t + qq
                                            mm(ps[bi_b][32 * qq:32 * qq + 32, 0:w_], wyr[:, q, tt, :],
                                               hbr[:, q, cb_:cb_ + w_], True, False,
                                               reads=["wyr", "hbr"], writes=[PS[bi_b]], inc=False, tp=(0, 32 * qq))
                                            mm(ps[bi_b][32 * qq:32 * qq + 32, 0:w_], wyi[:, q, tt, :],
                                               hbi[:, q, cb_:cb_ + w_], False, True,
                                               reads=["wyi", "hbi"], writes=[PS[bi_b]], inc=(qq == 3), tp=(0, 32 * qq))
                                        ytb, y2b, y3b = yt[b_], y2[b_], y3[b_]
                                        n1, n2, n3 = "yt%d" % b_, "y2%d" % b_, "y3%d" % b_
                                        act(ytb[:, 0:w_], ps[bi_b][:, 0:w_], AF.Copy, writes=[PS[bi_b], n1])
                                        vtt(ytb[:, 0:w_], ytb[:, 0:w_], ps[bi_a][:, 0:w_], ALU.add, reads=[n1],
                                            writes=[PS[bi_a], n1])
                                        vtt(y2b[:, 0:w_], ytb[:, 0:w_], ytb[:, 0:w_], ALU.mult, reads=[n1], writes=[n2])
                                        vts(y2b[:, 0:w_], y2b[:, 0:w_], 0.044715, 1.0, ALU.mult, ALU.add, reads=[n2],
                                            writes=[n2], eng="pool")
                                        vtt(y2b[:, 0:w_], y2b[:, 0:w_], ytb[:, 0:w_], ALU.mult, reads=[n1, n2],
                                            writes=[n2], eng="pool")
                                        act(y3b[:, 0:w_], y2b[:, 0:w_], AF.Sigmoid, scale=2.0 * math.sqrt(2.0 / math.pi),
                                            reads=[n2], writes=[n3])
                                        vtt(yb[:, t, tt, cb_:cb_ + w_], ytb[:, 0:w_], y3b[:, 0:w_], ALU.mult,
                                            reads=[n1, n3], writes=["yb"])

                            for t in range(4):
                                for pr_ in range(2):
                                    qs = (4 * t + 2 * pr_, 4 * t + 2 * pr_ + 1)
                                    g_mm(qs[0], 0)
                                    g_mm(qs[1], 1)
                                    ks_pair(qs)
                                if t >= 1:
                                    y_tile(t - 1)
                            y_tile(3)
                            fw.barrier()
                        with ExitStack() as S3:
                            wgl = sb("wgl", [P, 4, 512], BF16, S3)
                            fw.dma("sp", wgl[:], wglub.rearrange("(k p) n -> p k n", p=P), reads=["wglub"],
                                   writes=["wgl"])
                            sg = [sb("sg%d" % i, [P, 512], F32, S3) for i in range(2)]
                            yf = [sb("yf%d" % i, [P, L], BF16, S3) for i in range(2)]
                            ybf = yb[:].rearrange("p a b c -> p a (b c)")
                            it = 0
                            for m in range(4):
                                yfm = yf[m % 2]
                                yfn = "yf%d" % (m % 2)
                                for cb_ in range(0, L, 512):
                                    b_ = it % 2
                                    it += 1
                                    for k in range(4):
                                        mm(ps[b_][:], wgl[:, k, 128 * m:128 * m + 128], ybf[:, k, cb_:cb_ + 512],
                                           k == 0, k == 3, reads=["wgl", "yb"], writes=[PS[b_]], inc=(k == 3))
                                    act(sg[b_][:], ps[b_][:], AF.Sigmoid, bias=bglu[:, m:m + 1], reads=["bglu"],
                                        writes=[PS[b_], "sg%d" % b_])
                                    ntt = 512 // NC16 if NC16 <= 512 else 1
                                    if NC16 <= 512:
                                        tt0 = cb_ // NC16
                                        outap = yfm[:].rearrange("p (c t) -> p t c", t=TT)[:, tt0:tt0 + ntt, :]
                                        in0 = ybf[:, m, cb_:cb_ + 512].rearrange("p (t c) -> p t c", t=ntt)
                                        in1 = sg[b_][:].rearrange("p (t c) -> p t c", t=ntt)
                                    else:
                                        tt0 = cb_ // NC16
                                        c0 = cb_ % NC16
                                        outap = yfm[:].rearrange("p (c t) -> p t c", t=TT)[:, tt0, c0:c0 + 512]
                                        in0 = ybf[:, m, cb_:cb_ + 512]
                                        in1 = sg[b_][:]
                                    vtt(outap, in0, in1, ALU.mult, reads=["yb", "sg%d" % b_], writes=[yfn])
                                fw.dma("sp", cat_s[s, 4 + m], yfm[:], reads=[yfn], writes=["cat"])
                            fw.barrier()
            with ExitStack() as S:
                wdn = sb("wdn", [P, NJ, D], BF16, S)
                wo = sb("wo", [P, NK, D], BF16, S)
                wu = [sb("wu%d" % i, [P, NK, 256], BF16, S) for i in range(2)]
                g1 = sb("g1", [P, D], F32, S)
                a2 = sb("a2", [P, D], F32, S)
                s2 = sb("s2", [P, D], F32, S)
                g2 = sb("g2", [P, D], F32, S)
                xcs = [sb("xc%d" % i, [P, 4, D], F32, S) for i in range(2)]
                catc = sb("catc", [P, 8, 512], BF16, S)
                tmF = sb("tmF", [P, D], F32, S)
                tmB = tmF
                ubs = [sb("ub2_%d" % i, [P, D], BF16, S) for i in range(2)]
                uts = [sb("ut2_%d" % i, [P, NK, 512], BF16, S) for i in range(2)]
                at = sb("at", [P, NJ, 512], BF16, S)
                hb = [[sb("hb%d%d" % (a_, b_), [P, 514], BF16, S) for b_ in range(2)] for a_ in range(2)]
                sgb = [sb("sgb%d" % a_, [P, 512], F32, S) for a_ in range(2)]
                dg = [[[sb("dg%d%d%d" % (a_, b_, k_), [P, P], BF16, S) for k_ in range(3)] for b_ in range(2)]
                      for a_ in range(2)]
                carry = sb("carry", [P, 44, 2], BF16, S)
                tmpsF = [sb("F" + n_, sh_, F32, S) for n_, sh_ in (("st", [P, 12]), ("mv", [P, 2]), ("sd", [P, 1]),
                                                                  ("rstd", [P, 1]), ("nmr", [P, 1]))]
                tmpsB = [sb("B" + n_, sh_, F32, S) for n_, sh_ in (("st", [P, 12]), ("mv", [P, 2]), ("sd", [P, 1]),
                                                                  ("rstd", [P, 1]), ("nmr", [P, 1]))]
                lngb = sb("lngb", [P, 4, D], F32, S)
                fw.dma("sp", wo[:], woutb.rearrange("(k p) n -> p k n", p=P), reads=["woutb"], writes=["wo"])
                for t_, idx, nm in ((g1, 2, "g1"), (s2, 3, "s2"), (a2, 4, "a2"), (g2, 5, "g2")):
                    fw.dma("sp", t_[:], mods[s, idx], reads=["mods"], writes=[nm])
                for i in range(4):
                    fw.dma("sp", lngb[:, i, :], lng_d[i].partition_broadcast(P), writes=["lngb"])
                fw.op("pool", lambda h: h.memset(carry[:], 0.0), writes=["carry"])
                wcount = [0]
                wissued = {}

                def ln_affine(pf, tmps, tmx, src_ap, srcres, gi, bi, dst_ap, dstres):
                    st, mv, sd, rstd, nmr = tmps
                    ln_stats(S, src_ap, srcres, pf, st, mv, sd, rstd)
                    vstt(tmx[:], src_ap, mv[:, 0:1], lngb[:, gi, :], ALU.subtract, ALU.mult,
                         reads=[srcres, pf + "mv", "lngb"], writes=["Ftm"])
                    vstt(dst_ap, tmx[:], rstd[:, 0:1], lngb[:, bi, :], ALU.mult, ALU.add,
                         reads=["Ftm", pf + "rstd", "lngb"], writes=[dstres])

                def load(c):
                    tok0 = 512 * c
                    fw.dma("sp", xcs[c % 2][:], x_d[s, tok0:tok0 + 512, :].rearrange("(i p) d -> p i d", p=P),
                           writes=["xc%d" % (c % 2)])
                    fw.dma("sp", catc[:], cat_s[s, :, :, tok0:tok0 + 512].rearrange("k p t -> p k t"), reads=["cat"],
                           writes=["catc"])

                def store(c, i):
                    tok0 = 512 * c + 128 * i
                    fw.dma("sp", out_d[s, tok0:tok0 + 128, :], xcs[c % 2][:, i, :],
                           reads=["xc%d" % (c % 2)], writes=["out"])

                FBT = {0: (2, 7), 1: (2, 7), 2: (2, 7), 3: (5, 6)}

                def F_mm(c, i):
                    FB = FBT[i]
                    for hf in range(2):
                        for k in range(NK):
                            mm(ps[FB[hf]][:], catc[:, k, 128 * i:128 * i + 128], wo[:, k, 512 * hf:512 * hf + 512],
                               k == 0, k == NK - 1, reads=["catc", "wo"], writes=[PS[FB[hf]]], inc=(k == NK - 1))

                def F_chain(c, i, piece=None):
                    xc = xcs[c % 2]
                    xr = "xc%d" % (c % 2)
                    ub = ubs[i % 2]
                    ur = "ub2_%d" % (i % 2)
                    st, mv, sd, rstd, nmr = tmpsF
                    FB = FBT[i]
                    if piece in (None, 0):
                        for hf in range(2):
                            sl = slice(512 * hf, 512 * hf + 512)
                            vtt(tmF[:, sl], ps[FB[hf]][:], g1[:, sl], ALU.mult, reads=["g1"],
                                writes=[PS[FB[hf]], "Ftm"])
                        vstt(xc[:, i, :], xc[:, i, :], ALU_ALPHA, tmF[:], ALU.mult, ALU.add, reads=[xr, "Ftm"],
                             writes=[xr])
                    if piece in (None, 1):
                        ln_affine("F", tmpsF, tmF, xc[:, i, :], xr, 0, 1, xc[:, i, :], xr)
                    if piece in (None, 2):
                        ln_stats(S, xc[:, i, :], xr, "F", st, mv, sd, rstd)
                        vstt(tmF[:], xc[:, i, :], mv[:, 0:1], a2[:], ALU.subtract, ALU.mult,
                             reads=[xr, "Fmv", "a2"], writes=["Ftm"])
                        vstt(ub[:], tmF[:], rstd[:, 0:1], s2[:], ALU.mult, ALU.add, reads=["Ftm", "Frstd", "s2"],
                             writes=[ur])

                def F_T(c, i, tb=0):
                    ub = ubs[i % 2]
                    ur = "ub2_%d" % (i % 2)
                    ut = uts[c % 2]
                    pst = ps[tb][:].bitcast(BF16)
                    for k in range(NK):
                        fw.op("pe", lambda h: h.transpose(pst[:, 128 * k:128 * k + 128],
                                                          ub[:, 128 * k:128 * k + 128], identb[:]),
                              reads=[ur, "identb"], writes=[PS[tb]], inc=(k == NK - 1))
                    act(ut[:, :, 128 * i:128 * i + 128], pst.rearrange("p (k t) -> p k t", k=NK), AF.Copy,
                        writes=[PS[tb], "ut2_%d" % (c % 2)])

                def wu_dma(c, j):
                    if (c, j) in wissued or c >= NCH:
                        return
                    n_ = wcount[0]
                    wcount[0] += 1
                    wissued[(c, j)] = n_ % 2
                    fw.dma("sp", wu[n_ % 2][:], wupb2[j], reads=["wupb"], writes=["wu%d" % (n_ % 2)])

                def up(c):
                    prev = None
                    nxt = c + 1 < NCH
                    ut = uts[c % 2]
                    utn = "ut2_%d" % (c % 2)
                    for j in range(NJ):
                        wu_dma(c, j)
                        if j == 2 and nxt:
                            load(c + 1)
                        if nxt:
                            for i_ in range(2):
                                if j == 4 + 3 * i_:
                                    F_mm(c + 1, i_)
                                for pc_ in range(3):
                                    if j == 5 + 3 * i_ + pc_:
                                        F_chain(c + 1, i_, pc_)
                                if j == 11 + 3 * i_:
                                    F_T(c + 1, i_)
                        wb = wissued[(c, j)]
                        wt = wu[wb]
                        wn = "wu%d" % wb
                        b_ = j % 2
                        for vg in range(2):
                            bank = 3 + (2 * j + vg) % 3
                            for k in range(NK):
                                mm(ps[bank][:], wt[:, k, 128 * vg:128 * vg + 128], ut[:, k, :], k == 0, k == NK - 1,
                                   reads=[wn, utn], writes=[PS[bank]], inc=(k == NK - 1))
                            hbb = hb[b_][vg]
                            hh_ = "hbh%d%d" % (b_, vg)
                            hn = "hbb%d%d" % (b_, vg)
                            jj = j + NJ * vg
                            vcopy(hbb[:, 0:2], carry[:, jj, :], reads=["carry"], writes=[hh_], eng="pool")
                            act(hbb[:, 2:514], ps[bank][:], AF.Copy, writes=[PS[bank], hn])
                            vcopy(carry[:, jj, :], hbb[:, 512:514], reads=[hn], writes=["carry"], eng="pool")
                            for k in range(3):
                                vts(dg[b_][vg][k][:], identb[:], cw[:, k, jj:jj + 1], None, ALU.mult,
                                    reads=["identb", "cw"], writes=["dg%d%d" % (b_, vg)], eng="pool")
                        if prev is not None:
                            conv_gate(prev)
                        prev = j
                    conv_gate(prev)

                CB = (6, 1)

                def conv_gate(j):
                    b_ = j % 2
                    for vg in range(2):
                        hbb = hb[b_][vg]
                        hh_ = "hbh%d%d" % (b_, vg)
                        hn = "hbb%d%d" % (b_, vg)
                        for k in range(3):
                            mm(ps[CB[vg]][:], dg[b_][vg][k][:], hbb[:, k:k + 512], k == 0, k == 2,
                               reads=["dg%d%d" % (b_, vg), hh_, hn], writes=[PS[CB[vg]]], inc=(k == 2))
                    jg = j + NJ
                    act(sgb[b_][:], ps[CB[1]][:], AF.Silu, bias=cb[:, jg:jg + 1], reads=["cb"],
                        writes=[PS[CB[1]], "sgb%d" % b_])
                    vstt(at[:, j, :], ps[CB[0]][:], cb[:, j:j + 1], sgb[b_][:], ALU.add, ALU.mult,
                         reads=["cb", "sgb%d" % b_], writes=[PS[CB[0]], "at%d" % j])

                DB = ((0, 1), (3, 4))

                def down_mm(c, i):
                    for hf in range(2):
                        bk = DB[i % 2][hf]
                        for j in range(NJ):
                            mm(ps[bk][:], at[:, j, 128 * i:128 * i + 128], wdn[:, j, 512 * hf:512 * hf + 512],
                               j == 0, j == NJ - 1, reads=["at%d" % j, "wdn"], writes=[PS[bk]], inc=(j == NJ - 1))

                def ln2_chain(c, i):
                    xc = xcs[c % 2]
                    xr = "xc%d" % (c % 2)
                    for hf in range(2):
                        sl = slice(512 * hf, 512 * hf + 512)
                        bk = DB[i % 2][hf]
                        vtt(tmB[:, sl], ps[bk][:], g2[:, sl], ALU.mult, reads=["g2"], writes=[PS[bk], "Ftm"])
                    vstt(xc[:, i, :], xc[:, i, :], ALU_ALPHA, tmB[:], ALU.mult, ALU.add, reads=[xr, "Ftm"], writes=[xr])
                    ln_affine("B", tmpsB, tmB, xc[:, i, :], xr, 2, 3, xc[:, i, :], xr)

                load(0)
                fw.dma("sp", wdn[:], wdnb.rearrange("(j p) n -> p j n", p=P), reads=["wdnb"], writes=["wdn"])
                for i in range(4):
                    F_mm(0, i)
                    F_chain(0, i)
                    F_T(0, i)
                for c in range(NCH):
                    up(c)
                    wu_dma(c + 1, 0)
                    wu_dma(c + 1, 1)
                    nxt_ = c + 1 < NCH
                    for i in range(4):
                        down_mm(c, i)
                        if nxt_ and i == 0:
                            F_mm(c + 1, 2)
                            F_mm(c + 1, 3)
                        if nxt_ and i == 2:
                            F_T(c + 1, 2, tb=2)
                        if nxt_ and i == 3:
                            F_T(c + 1, 3, tb=2)
                        ln2_chain(c, i)
                        store(c, i)
                        if nxt_ and i == 0:
                            F_chain(c + 1, 2)
                        if nxt_ and i == 1:
                            F_chain(c + 1, 3)
                fw.barrier()
        fw.barrier()
    return nc, dbg_outs, fw.rec


ALU_ALPHA = ALPHA


def _host_inputs(inputs, core, NSEQ, L):
    f = lambda a: np.ascontiguousarray(np.asarray(a, dtype=np.float32))
    b0 = core * NSEQ
    x = f(inputs["x"])[b0:b0 + NSEQ, :L]
    c = f(inputs["c"])[b0:b0 + NSEQ]
    m = {}
    m["x"] = f(x)
    m["cT"] = f(c.reshape(NSEQ, NK, P).transpose(2, 0, 1))
    m["w_ada"] = f(inputs["w_ada"][0])
    m["b_ada"] = f(inputs["b_ada"][0])
    m["w_in"] = f(inputs["w_in"][0])
    m["w_out"] = f(inputs["w_out"][0])
    m["w_up"] = f(inputs["w_up"][0])
    m["w_down"] = f(inputs["w_down"][0])
    m["w_glu"] = f(inputs["w_glu"][0])
    m["ln_gb"] = f(np.stack([inputs["ln1_g"][0], inputs["ln1_b"][0], inputs["ln2_g"][0], inputs["ln2_b"][0]]))
    cwv = f(inputs["conv_w"][0])
    m["convw"] = f(cwv.reshape(3, 44, P).transpose(2, 0, 1))
    m["convb"] = f(f(inputs["conv_b"][0]).reshape(44, P).T)
    m["bglu"] = f(f(inputs["b_glu"][0]).reshape(4, P).T)
    m["subg"] = f(inputs["subln_g"][0])
    m["lamv"] = f(np.concatenate([inputs["lam_q1"][0], inputs["lam_k1"][0], inputs["lam_q2"][0], inputs["lam_k2"][0]]))
    m["relb"] = f(np.asarray(inputs["rel_bias"]).reshape(-1))
    are, aim = f(inputs["ssm_a_re"][0]), f(inputs["ssm_a_im"][0])
    ldt = f(inputs["ssm_log_dt"][0])
    bre, bim = f(inputs["ssm_b_re"][0]), f(inputs["ssm_b_im"][0])
    cre, cim = f(inputs["ssm_c_re"][0]), f(inputs["ssm_c_im"][0])
    dsk = f(inputs["ssm_d"][0])
    rep = lambda a: np.repeat(a.reshape(4, 8, 1, -1), 16, axis=2).reshape(4, P, -1).transpose(1, 0, 2)
    m["ar1"] = f(rep(are))
    m["ai1"] = f(rep(aim))
    m["ldt1"] = f(rep(ldt.reshape(32, 1))[:, :, 0])
    m["br1"] = f(bre.transpose(0, 2, 1).reshape(4, P, 64).transpose(1, 0, 2))
    m["bi1"] = f(bim.transpose(0, 2, 1).reshape(4, P, 64).transpose(1, 0, 2))
    m["d1"] = f(dsk.reshape(4, P).T)
    ctr = cre.reshape(4, 8, 16, 64).transpose(3, 0, 1, 2).reshape(64, 4, P)
    cti = cim.reshape(4, 8, 16, 64).transpose(3, 0, 1, 2).reshape(64, 4, P)
    m["ct"] = f(np.concatenate([ctr, cti], axis=0))
    l2 = lambda a: a.reshape(16, 2, 64).transpose(1, 2, 0).reshape(P, 16)
    m["ar2"] = f(l2(are))
    m["ai2"] = f(l2(aim))
    m["ldt2"] = f(l2(np.repeat(ldt.reshape(32, 1), 64, axis=1)))
    m["c2r"] = f(cre.reshape(16, 2, 16, 64).transpose(1, 3, 0, 2).reshape(P, 16, 16))
    m["c2i"] = f(cim.reshape(16, 2, 16, 64).transpose(1, 3, 0, 2).reshape(P, 16, 16))
    m["ident"] = np.eye(P, dtype=np.float32)
    pidx = np.arange(P)
    gp = pidx // 16
    m["mpar"] = f(np.stack([(gp % 2 == 0), (gp % 2 == 1)], axis=1))
    mb = np.zeros((P, 8, 16), np.float32)
    mb[pidx, gp, :] = 1.0
    m["mblk"] = mb
    return m


_CACHE = {}


def kernel(**inputs):
    NSEQ, L, NCORES = 2, 4096, 8
    if "nc" not in _CACHE:
        _CACHE["nc"] = build(L, NSEQ)[0]
    nc = _CACHE["nc"]
    in_maps = [_host_inputs(inputs, c, NSEQ, L) for c in range(NCORES)]
    res = run_bass_kernel_spmd(nc, in_maps, core_ids=list(range(NCORES)))
    outs = [np.asarray(r["out"], dtype=np.float32) for r in res.results]
    return np.concatenate(outs, axis=0)
```

```python
import math
from contextlib import ExitStack
import numpy as np
import concourse.bass as bass
import concourse.mybir as mybir
from concourse.bass_utils import run_bass_kernel_spmd

F32 = mybir.dt.float32
BF16 = mybir.dt.bfloat16
I32 = mybir.dt.int32
AF = mybir.ActivationFunctionType
ALU = mybir.AluOpType
AX = mybir.AxisListType

P = 128
D = 1024
NK = 8
DFF = 2816
NJ = 22
TT = 16
EPS = 1e-5
ALPHA = 2.0 ** 0.25
LAM_INIT = 0.2
NDS = 24
TWO_PI = 2.0 * math.pi


class _E:
    def __init__(self, name, h, sem):
        self.name, self.h, self.sem, self.count, self.waited = name, h, sem, 0, {}
        self.seq = 0
        self.cmap = {}


class FW:
    def __init__(self, nc, needed=None):
        self.nc = nc
        self.needed = needed
        self.rec = set()
        self.eng = {}
        self.sems = {}
        for name, h in [("pe", nc.tensor), ("dve", nc.vector), ("act", nc.scalar),
                        ("pool", nc.gpsimd), ("sp", nc.sync)]:
            s = nc.alloc_semaphore(name="sem_" + name)
            self.eng[name] = _E(name, h, s)
            self.sems[name] = s
        for i in range(NDS):
            self.sems["d%d" % i] = nc.alloc_semaphore(name="dsem%d" % i)
        self.dcount = 0
        self.lastw = {}
        self.readers = {}

    def _wait(self, e, tk):
        if tk is None:
            return
        key, val = tk
        if e.waited.get(key, 0) >= val:
            return
        if key in self.eng:
            self.rec.add((key, val))
            e.h.wait_ge(self.sems[key], self.eng[key].cmap[val])
        else:
            e.h.wait_ge(self.sems[key], val)
        e.waited[key] = val

    def _deps(self, e, reads, writes, same_engine_all):
        for r in reads:
            self._wait(e, self.lastw.get(r))
        for w in writes:
            t = self.lastw.get(w)
            if t is not None and (same_engine_all or t[0] != e.name):
                self._wait(e, t)
            for k, v in self.readers.get(w, {}).items():
                if same_engine_all or k != e.name:
                    self._wait(e, (k, v))

    def _record(self, tk, reads, writes):
        for r in reads:
            d = self.readers.setdefault(r, {})
            if d.get(tk[0], 0) < tk[1]:
                d[tk[0]] = tk[1]
        for w in writes:
            self.lastw[w] = tk
            self.readers[w] = {}

    def op(self, ename, fn, reads=(), writes=(), inc=True):
        e = self.eng[ename]
        self._deps(e, reads, writes, False)
        inst = fn(e.h)
        if inc:
            e.seq += 1
            tk = (e.name, e.seq)
            if self.needed is None or tk in self.needed:
                e.count += 1
                inst.then_inc(e.sem, 1)
                e.cmap[e.seq] = e.count
        else:
            tk = (e.name, e.seq + 1)
        self._record(tk, reads, writes)
        return inst

    def dma(self, qname, out, in_, reads=(), writes=()):
        e = self.eng[qname]
        i = self.dcount
        self.dcount += 1
        key = "d%d" % (i % NDS)
        val = 16 * (i // NDS + 1)
        if i >= NDS:
            self._wait(e, (key, val - 16))
        self._deps(e, reads, writes, True)
        e.h.dma_start(out=out, in_=in_).then_inc(self.sems[key], 16)
        self._record((key, val), reads, writes)

    def barrier(self):
        engs = list(self.eng.values())
        for e in engs:
            for f in engs:
                if f is not e and f.seq > 0:
                    self._wait(e, (f.name, f.seq))
            for i in range(min(self.dcount, NDS)):
                n = (self.dcount - 1 - i) // NDS + 1
                self._wait(e, ("d%d" % i, 16 * n))
        self.lastw = {}
        self.readers = {}


def _t5_thresholds():
    d = np.arange(0, 300, dtype=np.int32)
    df = np.maximum(d, 1).astype(np.float32)
    large = 16 + (np.log(df / np.float32(16)) / np.float32(math.log(128 / 16)) * np.float32(16)).astype(np.int32)
    large = np.minimum(large, 31)
    bucket = np.where(d < 16, d, large)
    return [int(np.argmax(bucket >= b)) for b in range(1, 32)]


def build(L, NSEQ, dbg=False):
    _, _, rec = _build(L, NSEQ, dbg, None)
    nc, dbg_outs, _ = _build(L, NSEQ, dbg, rec)
    return nc, dbg_outs


def _build(L, NSEQ, dbg, needed):
    NCH = L // 512
    NB = L // 128
    NC16 = L // TT
    nc = bass.Bass("TRN2", target_bir_lowering=False)
    fw = FW(nc, needed)
    dbg_outs = {}

    def din(name, shape, dt=F32):
        return nc.dram_tensor(name, list(shape), dt, kind="ExternalInput").ap()

    def dscr(name, shape, dt):
        return nc.dram_tensor(name, list(shape), dt, kind="Internal").ap()

    x_d = din("x", [NSEQ, L, D])
    cT_d = din("cT", [P, NSEQ, NK])
    wada_d = din("w_ada", [D, 6 * D])
    bada_d = din("b_ada", [6 * D])
    win_d = din("w_in", [D, 2048])
    wout_d = din("w_out", [D, D])
    wup_d = din("w_up", [D, 2 * DFF])
    wdn_d = din("w_down", [DFF, D])
    wglu_d = din("w_glu", [512, 512])
    lng_d = din("ln_gb", [4, D])
    convw_d = din("convw", [P, 3, 44])
    convb_d = din("convb", [P, 44])
    bglu_d = din("bglu", [P, 4])
    subg_d = din("subg", [P])
    lamv_d = din("lamv", [4 * 64])
    relb_d = din("relb", [128])
    ar1_d = din("ar1", [P, 4, 64])
    ai1_d = din("ai1", [P, 4, 64])
    ldt1_d = din("ldt1", [P, 4])
    br1_d = din("br1", [P, 4, 64])
    bi1_d = din("bi1", [P, 4, 64])
    d1_d = din("d1", [P, 4])
    ct_d = din("ct", [P, 4, P])
    ar2_d = din("ar2", [P, 16])
    ai2_d = din("ai2", [P, 16])
    ldt2_d = din("ldt2", [P, 16])
    c2r_d = din("c2r", [P, 16, 16])
    c2i_d = din("c2i", [P, 16, 16])
    ident_d = din("ident", [P, P])
    mpar_d = din("mpar", [P, 2])
    mblk_d = din("mblk", [P, 8, 16])
    out_d = nc.dram_tensor("out", [NSEQ, L, D], F32, kind="ExternalOutput").ap()

    winb = dscr("winb", [D, 2048], BF16)
    woutb = dscr("woutb", [D, D], BF16)
    wupb2 = dscr("wupb2", [NJ, P, NK, 256], BF16)
    wdnb = dscr("wdnb", [DFF, D], BF16)
    wglub = dscr("wglub", [512, 512], BF16)
    wadab = dscr("wadab", [D, 6 * D], BF16)
    mods = dscr("mods", [NSEQ, 6, P, D], F32)
    wk_s = dscr("wk_s", [P, 4 * 16 * 128], BF16)
    wgr_s = dscr("wgr_s", [P, 4 * 16 * 128], BF16)
    wgi_s = dscr("wgi_s", [P, 4 * 16 * 128], BF16)
    wyr_s = dscr("wyr_s", [P, 16 * 16 * 32], BF16)
    wyi_s = dscr("wyi_s", [P, 16 * 16 * 32], BF16)
    cat_s = dscr("cat_s", [NSEQ, 8, P, L], BF16)
    dbias_s = dscr("dbias_s", [P, 4 * 2 * P], F32)

    def dbg_dump(name, ap_sb, shape, res, dt=F32):
        if not dbg:
            return
        t = nc.dram_tensor("dbg_" + name, list(shape), dt, kind="ExternalOutput").ap()
        dbg_outs[name] = t
        fw.dma("sp", t, ap_sb, reads=[res], writes=["dbg_" + name])

    V = lambda f: (lambda h: f(h))

    def vts(out, in0, s1, s2, op0, op1=None, reads=(), writes=(), eng="dve"):
        if op1 is None:
            fw.op(eng, lambda h: h.tensor_scalar(out=out, in0=in0, scalar1=s1, scalar2=None, op0=op0),
                  reads, writes)
        else:
            fw.op(eng, lambda h: h.tensor_scalar(out=out, in0=in0, scalar1=s1, scalar2=s2, op0=op0, op1=op1),
                  reads, writes)

    def vtt(out, in0, in1, op, reads=(), writes=(), eng="dve"):
        fw.op(eng, lambda h: h.tensor_tensor(out=out, in0=in0, in1=in1, op=op), reads, writes)

    def vstt(out, in0, s, in1, op0, op1, reads=(), writes=()):
        fw.op("dve", lambda h: h.scalar_tensor_tensor(out=out, in0=in0, scalar=s, in1=in1, op0=op0, op1=op1),
              reads, writes)

    def vcopy(out, in_, reads=(), writes=(), eng="dve"):
        fw.op(eng, lambda h: h.tensor_copy(out=out, in_=in_), reads, writes)

    def act(out, in_, func, bias=None, scale=None, reads=(), writes=()):
        kw = {}
        if bias is not None:
            kw["bias"] = bias
        if scale is not None:
            kw["scale"] = scale
        fw.op("act", lambda h: h.activation(out=out, in_=in_, func=func, **kw), reads, writes)

    def mm(out, lhsT, rhs, start, stop, reads=(), writes=(), inc=True, tp=None):
        if tp is None:
            fw.op("pe", lambda h: h.matmul(out, lhsT=lhsT, rhs=rhs, start=start, stop=stop,
                                           skip_group_check=True), reads, writes, inc=inc)
        else:
            fw.op("pe", lambda h: h.matmul(out, lhsT=lhsT, rhs=rhs, start=start, stop=stop,
                                           skip_group_check=True, tile_position=tp), reads, writes, inc=inc)

    with ExitStack() as G:
        _uid = [0]

        def sb(name, shape, dt, stack=G):
            _uid[0] += 1
            return stack.enter_context(nc.sbuf_tensor("s%d_%s" % (_uid[0], name), list(shape), dt))

        ps = [G.enter_context(nc.psum_tensor("ps%d" % i, [P, 512], F32)) for i in range(8)]
        PS = ["ps%d" % i for i in range(8)]

        identb = sb("identb", [P, P], BF16)
        identf = sb("identf", [P, P], F32)
        g08 = sb("g08", [P, P], F32)
        lamneg = sb("lamneg", [P, 1], F32)
        cw = sb("cw", [P, 3, 44], F32)
        cb = sb("cb", [P, 44], F32)
        bglu = sb("bglu", [P, 4], F32)
        apr = sb("apr", [P, 16, 8], F32)
        api = sb("api", [P, 16, 8], F32)
        napi = sb("napi", [P, 16, 8], F32)

        fw.dma("sp", identf[:], ident_d, writes=["identf"])
        fw.dma("pool", identb[:], ident_d, writes=["identb"])
        fw.dma("sp", cw[:], convw_d, writes=["cw"])
        fw.dma("sp", cb[:], convb_d, writes=["cb"])
        fw.dma("sp", bglu[:], bglu_d, writes=["bglu"])

        mhalf = sb("mhalf", [P, 16], F32)
        SP0 = ExitStack()
        SP0.__enter__()
        lq = sb("lq", [P, 4, 64], F32, SP0)
        rb = sb("rb", [P, 32, 4], F32, SP0)
        sh1 = [P, 4, 64]
        ar1 = sb("ar1", sh1, F32, SP0)
        ai1 = sb("ai1", sh1, F32, SP0)
        br1 = sb("br1", sh1, F32, SP0)
        bi1 = sb("bi1", sh1, F32, SP0)
        ldt1 = sb("ldt1", [P, 4], F32, SP0)
        d1 = sb("d1", [P, 4], F32, SP0)
        ct = sb("ct", [P, 4, P], F32, SP0)
        mpar = sb("mpar", [P, 2], F32, SP0)
        mblk = sb("mblk", [P, 8, 16], F32, SP0)
        sh2 = [P, 16]
        ar2 = sb("ar2", sh2, F32, SP0)
        ai2 = sb("ai2", sh2, F32, SP0)
        ldt2 = sb("ldt2", sh2, F32, SP0)
        c2r = sb("c2r", [P, 16, 16], F32, SP0)
        c2i = sb("c2i", [P, 16, 16], F32, SP0)
        dist = sb("dist", [P, 2, P], F32, SP0)
        fw.op("pool", lambda h: h.iota(dist[:, 0, :], pattern=[[1, P]], base=0, channel_multiplier=-1,
                                       allow_small_or_imprecise_dtypes=True), writes=["dist"])
        fw.op("pool", lambda h: h.iota(dist[:, 1, :], pattern=[[1, P]], base=128, channel_multiplier=-1,
                                       allow_small_or_imprecise_dtypes=True), writes=["dist"])
        fw.op("pool", lambda h: h.memset(mhalf[:], -0.5), writes=["mhalf"])
        fw.dma("sp", lq[:].rearrange("p a b -> p (a b)"), lamv_d.partition_broadcast(P), writes=["lq"])
        fw.dma("sp", g08[:], subg_d.partition_broadcast(P), writes=["g08"])
        fw.dma("sp", rb[:].rearrange("p a b -> p (a b)"), relb_d.partition_broadcast(P), writes=["rb"])
        for t_, d_ in [(ar1, ar1_d), (ai1, ai1_d), (br1, br1_d), (bi1, bi1_d), (ldt1, ldt1_d), (d1, d1_d),
                       (ct, ct_d), (mpar, mpar_d), (mblk, mblk_d)]:
            fw.dma("sp", t_[:], d_, writes=["l1in"])
            fw.lastw.pop("l1in_prev", None)
        for t_, d_ in [(ar2, ar2_d), (ai2, ai2_d), (ldt2, ldt2_d), (c2r, c2r_d), (c2i, c2i_d)]:
            fw.dma("sp", t_[:], d_, writes=["l2in"])
        def cast_w(src, dst, rows, cols, name):
            tot = rows * cols
            per = tot // P
            s2 = src.rearrange("(p r) n -> p (r n)", p=P)
            d2 = dst.rearrange("(p r) n -> p (r n)", p=P)
            step = 8192
            for o in range(0, per, step):
                w = min(step, per - o)
                fw.dma("pool", d2[:, o:o + w], s2[:, o:o + w], reads=["lq", "rb", "l1in", "l2in"],
                       writes=[name])

        cast_w(wada_d, wadab, D, 6 * D, "wadab")
        cast_w(win_d, winb, D, 2048, "winb")
        cast_w(wout_d, woutb, D, D, "woutb")
        cast_w(wglu_d, wglub, 512, 512, "wglub")
        for j in range(NJ):
            for vg in range(2):
                c0 = vg * DFF + 128 * j
                fw.dma("pool", wupb2[j, :, :, 128 * vg:128 * vg + 128],
                       wup_d[:, c0:c0 + 128].rearrange("(k p) n -> p k n", p=P),
                       reads=["lq", "rb", "l1in", "l2in"], writes=["wupb"])
        cast_w(wdn_d, wdnb, DFF, D, "wdnb")

        with ExitStack() as S:
            dbias = sb("dbias", [P, 4, 2, P], F32, S)
            lt = sb("lt", [P, 2, 64], F32, S)
            ls = sb("ls", [P, 2], F32, S)
            le = sb("le", [P, 2], F32, S)
            dl = sb("dl", [P, 32, 4], F32, S)
            acc = sb("acc", [P, P], F32, S)
            tmp = sb("tmpb", [P, P], F32, S)
            vtt(lt[:, 0, :], lq[:, 0, :], lq[:, 1, :], ALU.mult, reads=["lq"], writes=["lt"])
            vtt(lt[:, 1, :], lq[:, 2, :], lq[:, 3, :], ALU.mult, reads=["lq"], writes=["lt"])
            fw.op("dve", lambda h: h.tensor_reduce(out=ls[:], in_=lt[:], axis=AX.X, op=ALU.add),
                  reads=["lt"], writes=["ls"])
            act(le[:], ls[:], AF.Exp, reads=["ls"], writes=["le"])
            vtt(lamneg[:], le[:, 1:2], le[:, 0:1], ALU.subtract, reads=["le"], writes=["lamneg"])
            vts(lamneg[:], lamneg[:], -LAM_INIT, None, ALU.add, reads=["lamneg"], writes=["lamneg"])
            vts(g08[:], g08[:], 1.0 - LAM_INIT, None, ALU.mult, reads=["g08"], writes=["g08"])
            vtt(dl[:, 1:32, :], rb[:, 1:32, :], rb[:, 0:31, :], ALU.subtract, reads=["rb"], writes=["dl"])
            vtt(dl[:, 0, :], rb[:, 0, :], rb[:, 31, :], ALU.subtract, reads=["rb"], writes=["dl"])
            thr = _t5_thresholds()
            for hh in range(4):
                for dd in range(2):
                    vts(acc[:], dist[:, dd, :], 0.0, dl[:, 0, hh:hh + 1], ALU.mult, ALU.add,
                        reads=["dist", "dl"], writes=["acc"])
                    for b in range(1, 32):
                        vts(tmp[:], dist[:, dd, :], float(thr[b - 1]) - 0.5, dl[:, b, hh:hh + 1],
                            ALU.is_ge, ALU.mult, reads=["dist", "dl"], writes=["tmpb"])
                        vtt(acc[:], acc[:], tmp[:], ALU.add, reads=["acc", "tmpb"], writes=["acc"])
                    if dd == 0:
                        vts(tmp[:], dist[:, 0, :], -0.5, -1e30, ALU.is_lt, ALU.mult, reads=["dist"], writes=["tmpb"])
                        vstt(dbias[:, hh, 0, :], acc[:], 8.0, tmp[:], ALU.mult, ALU.add,
                             reads=["acc", "tmpb"], writes=["dbias"])
                    else:
                        vts(dbias[:, hh, 1, :], acc[:], 8.0, None, ALU.mult, reads=["acc"], writes=["dbias"])
            fw.dma("sp", dbias_s, dbias[:].rearrange("p a b c -> p (a b c)"), reads=["dbias"], writes=["dbias_s"])
            fw.barrier()

        def sincos(S, name, ang, shape, res):
            outs = []
            for idx, shift in enumerate((0.0, math.pi / 2)):
                a = sb("%s_a%d" % (name, idx), shape, F32, S)
                ki = sb("%s_k%d" % (name, idx), shape, I32, S)
                kf = sb("%s_f%d" % (name, idx), shape, F32, S)
                r = sb("%s_r%d" % (name, idx), shape, F32, S)
                t = sb("%s_t%d" % (name, idx), shape, F32, S)
                o = sb("%s_o%d" % (name, idx), shape, F32, S)
                rn = "%s_r%d" % (name, idx)
                vts(a[:], ang, shift, None, ALU.add, reads=[res], writes=[rn + "a"])
                vts(t[:], a[:], 1.0 / TWO_PI, None, ALU.mult, reads=[rn + "a"], writes=[rn + "t"])
                vcopy(ki[:], t[:], reads=[rn + "t"], writes=[rn + "k"])
                vcopy(kf[:], ki[:], reads=[rn + "k"], writes=[rn + "f"])
                vstt(r[:], kf[:], -TWO_PI, a[:], ALU.mult, ALU.add, reads=[rn + "f", rn + "a"], writes=[rn])
                vts(t[:], r[:], math.pi, -TWO_PI, ALU.is_gt, ALU.mult, reads=[rn], writes=[rn + "t"])
                vtt(r[:], r[:], t[:], ALU.add, reads=[rn, rn + "t"], writes=[rn])
                vts(t[:], r[:], -math.pi, TWO_PI, ALU.is_lt, ALU.mult, reads=[rn], writes=[rn + "t"])
                vtt(r[:], r[:], t[:], ALU.add, reads=[rn, rn + "t"], writes=[rn])
                act(o[:], r[:], AF.Sin, reads=[rn], writes=[rn + "o"])
                outs.append((o, rn + "o"))
            return outs

        def abar(S, name, ar, ai, ldt_bc, shape, res):
            dtt = sb(name + "_dt", shape, F32, S)
            xx = sb(name + "_x", shape, F32, S)
            mag = sb(name + "_mag", shape, F32, S)
            ang = sb(name + "_ang", shape, F32, S)
            abr = sb(name + "_abr", shape, F32, S)
            abi = sb(name + "_abi", shape, F32, S)
            act(dtt[:], ldt_bc, AF.Exp, reads=[res], writes=[name + "dt"])
            vtt(xx[:], ar, dtt[:], ALU.mult, reads=[res, name + "dt"], writes=[name + "x"])
            act(mag[:], xx[:], AF.Exp, reads=[name + "x"], writes=[name + "mag"])
            vtt(ang[:], ai, dtt[:], ALU.mult, reads=[res, name + "dt"], writes=[name + "ang"])
            (sn, snr), (cs, csr) = sincos(S, name + "sc", ang[:], shape, name + "ang")
            vtt(abr[:], mag[:], cs[:], ALU.mult, reads=[name + "mag", csr], writes=[name + "abr"])
            vtt(abi[:], mag[:], sn[:], ALU.mult, reads=[name + "mag", snr], writes=[name + "abi"])
            return abr, abi

        with ExitStack() as S:
            sh1 = [P, 4, 64]
            vts(ct[64:128], ct[64:128], -1.0, None, ALU.mult, reads=["l1in"], writes=["ct"])
            shz = [P, 4, 16, 64]
            zc = sb("zc", [P, 4, 16, P], F32, S)
            zr = zc[:, :, :, 0:64]
            zi = zc[:, :, :, 64:128]
            zt = sb("zt", shz, F32, S)
            with ExitStack() as SA:
                abr, abi = abar(SA, "a1", ar1[:], ai1[:], ldt1[:].unsqueeze(2).to_broadcast(sh1), sh1, "l1in")
                RA = ["a1abr", "a1abi", "l1in"]
                den = sb("den", sh1, F32, SA)
                t1 = sb("t1s", sh1, F32, SA)
                t2 = sb("t2s", sh1, F32, SA)
                fr = sb("fr", sh1, F32, SA)
                fi = sb("fi", sh1, F32, SA)
                bbr = sb("bbr", sh1, F32, SA)
                bbi = sb("bbi", sh1, F32, SA)
                am1 = sb("am1", sh1, F32, SA)
                vtt(den[:], ar1[:], ar1[:], ALU.mult, reads=RA, writes=["den"])
                vtt(t1[:], ai1[:], ai1[:], ALU.mult, reads=RA, writes=["t1s"])
                vtt(den[:], den[:], t1[:], ALU.add, reads=["den", "t1s"], writes=["den"])
                fw.op("dve", lambda h: h.reciprocal(out=den[:], in_=den[:]), reads=["den"], writes=["den"])
                vts(am1[:], abr[:], -1.0, None, ALU.add, reads=RA, writes=["am1"])
                vtt(t1[:], am1[:], ar1[:], ALU.mult, reads=["am1", "l1in", "den"], writes=["t1s"])
                vtt(t2[:], abi[:], ai1[:], ALU.mult, reads=RA, writes=["t2s"])
                vtt(t1[:], t1[:], t2[:], ALU.add, reads=["t1s", "t2s"], writes=["t1s"])
                vtt(fr[:], t1[:], den[:], ALU.mult, reads=["t1s", "den"], writes=["fr"])
                vtt(t1[:], abi[:], ar1[:], ALU.mult, reads=RA + ["fr"], writes=["t1s"])
                vtt(t2[:], am1[:], ai1[:], ALU.mult, reads=["am1", "l1in", "fr"], writes=["t2s"])
                vtt(t1[:], t1[:], t2[:], ALU.subtract, reads=["t1s", "t2s"], writes=["t1s"])
                vtt(fi[:], t1[:], den[:], ALU.mult, reads=["t1s", "den"], writes=["fi"])
                vtt(t1[:], fr[:], br1[:], ALU.mult, reads=["fr", "l1in", "fi"], writes=["t1s"])
                vtt(t2[:], fi[:], bi1[:], ALU.mult, reads=["fi", "l1in"], writes=["t2s"])
                vtt(bbr[:], t1[:], t2[:], ALU.subtract, reads=["t1s", "t2s"], writes=["bbr"])
                vtt(t1[:], fr[:], bi1[:], ALU.mult, reads=["fr", "l1in", "bbr"], writes=["t1s"])
                vtt(t2[:], fi[:], br1[:], ALU.mult, reads=["fi", "l1in", "bbr"], writes=["t2s"])
                vtt(bbi[:], t1[:], t2[:], ALU.add, reads=["t1s", "t2s"], writes=["bbi"])
                pwr = sb("pwr", [P, 4, 17, 64], F32, SA)
                pwi = sb("pwi", [P, 4, 17, 64], F32, SA)
                fw.op("dve", lambda h: h.memset(pwr[:, :, 0, :], 1.0), writes=["pw"])
                fw.op("dve", lambda h: h.memset(pwi[:, :, 0, :], 0.0), writes=["pw"])
                for j in range(16):
                    vtt(t1[:], pwr[:, :, j, :], abr[:], ALU.mult, reads=["pw"] + RA, writes=["t1s"])
                    vtt(t2[:], pwi[:, :, j, :], abi[:], ALU.mult, reads=["pw"] + RA, writes=["t2s"])
                    vtt(pwr[:, :, j + 1, :], t1[:], t2[:], ALU.subtract, reads=["t1s", "t2s"], writes=["pw"])
                    vtt(t1[:], pwr[:, :, j, :], abi[:], ALU.mult, reads=["pw"] + RA, writes=["t1s"])
                    vtt(t2[:], pwi[:, :, j, :], abr[:], ALU.mult, reads=["pw"] + RA, writes=["t2s"])
                    vtt(pwi[:, :, j + 1, :], t1[:], t2[:], ALU.add, reads=["t1s", "t2s"], writes=["pw"])
                bbr_b = bbr[:].unsqueeze(2).to_broadcast(shz)
                bbi_b = bbi[:].unsqueeze(2).to_broadcast(shz)
                vtt(zr[:], pwr[:, :, 0:16, :], bbr_b, ALU.mult, reads=["pw", "bbr"], writes=["zr"])
                vtt(zt[:], pwi[:, :, 0:16, :], bbi_b, ALU.mult, reads=["pw", "bbi"], writes=["zt"])
                vtt(zr[:], zr[:], zt[:], ALU.subtract, reads=["zr", "zt"], writes=["zr"])
                vtt(zi[:], pwr[:, :, 0:16, :], bbi_b, ALU.mult, reads=["pw", "bbi", "zr"], writes=["zi"])
                vtt(zt[:], pwi[:, :, 0:16, :], bbr_b, ALU.mult, reads=["pw", "bbr", "zr"], writes=["zt"])
                vtt(zi[:], zi[:], zt[:], ALU.add, reads=["zi", "zt"], writes=["zi"])
                fw.barrier()
            with ExitStack() as SB:
                wg_r = sb("wg_r", [P, 4, 16, P], BF16, SB)
                wg_i = sb("wg_i", [P, 4, 16, P], BF16, SB)
                for tau in range(16):
                    for (wg, zz, rn) in ((wg_r, zr, "zr"), (wg_i, zi, "zi")):
                        for par in range(2):
                            vts(wg[:, :, tau, 64 * par:64 * par + 64], zz[:, :, 15 - tau, :], mpar[:, par:par + 1], None,
                                ALU.mult, reads=[rn, "l1in"], writes=["wg"])
                fw.dma("sp", wgr_s, wg_r[:].rearrange("p a b c -> p (a b c)"), reads=["wg"], writes=["wgs"])
                fw.dma("sp", wgi_s, wg_i[:].rearrange("p a b c -> p (a b c)"), reads=["wg"], writes=["wgs"])
                fw.barrier()
            with ExitStack() as SC:
                wk = sb("wk", [P, 4, 16, P], BF16, SC)
                zT = [sb("zT%d" % i, [P, P], F32, SC) for i in range(2)]
                ktmp = sb("ktmp", [P, P], F32, SC)
                m128 = mblk[:].rearrange("p g h -> p (g h)")
                it = 0
                for t in range(4):
                    for j in range(16):
                        b_ = it % 2
                        it += 1
                        fw.op("pe", lambda h: h.transpose(ps[b_][:, 0:P], zc[:, t, j, :], identf[:]),
                              reads=["zr", "zi", "identf"], writes=[PS[b_]])
                        act(zT[b_][:], ps[b_][:, 0:P], AF.Copy, writes=[PS[b_], "zT%d" % b_])
                        mm(ps[2 + b_][:, 0:P], zT[b_][:], ct[:, t, :], True, True, reads=["zT%d" % b_, "ct"],
                           writes=[PS[2 + b_]])
                        if j == 0:
                            vtt(ktmp[:], ps[2 + b_][:, 0:P], m128, ALU.mult, reads=["l1in"],
                                writes=[PS[2 + b_], "ktmp"])
                            vstt(wk[:, t, j, :], identf[:], d1[:, t:t + 1], ktmp[:], ALU.mult, ALU.add,
                                 reads=["identf", "l1in", "ktmp"], writes=["wk"])
                        else:
                            vtt(wk[:, t, j, :], ps[2 + b_][:, 0:P], m128, ALU.mult, reads=["l1in"],
                                writes=[PS[2 + b_], "wk"])
                fw.dma("sp", wk_s, wk[:].rearrange("p a b c -> p (a b c)"), reads=["wk"], writes=["wks"])
                fw.barrier()
        with ExitStack() as S:
            sh2 = [P, 16]
            abr, abi = abar(S, "a2", ar2[:], ai2[:], ldt2[:], sh2, "l2in")
            RA = ["a2abr", "a2abi"]
            p2r = sb("p2r", [P, 16, 17], F32, S)
            p2i = sb("p2i", [P, 16, 17], F32, S)
            u1 = sb("u1", sh2, F32, S)
            u2 = sb("u2", sh2, F32, S)
            fw.op("dve", lambda h: h.memset(p2r[:, :, 0], 1.0), writes=["p2"])
            fw.op("dve", lambda h: h.memset(p2i[:, :, 0], 0.0), writes=["p2"])
            for j in range(16):
                vtt(u1[:], p2r[:, :, j], abr[:], ALU.mult, reads=["p2"] + RA, writes=["u1"])
                vtt(u2[:], p2i[:, :, j], abi[:], ALU.mult, reads=["p2"] + RA, writes=["u2"])
                vtt(p2r[:, :, j + 1], u1[:], u2[:], ALU.subtract, reads=["u1", "u2"], writes=["p2"])
                vtt(u1[:], p2r[:, :, j], abi[:], ALU.mult, reads=["p2"] + RA, writes=["u1"])
                vtt(u2[:], p2i[:, :, j], abr[:], ALU.mult, reads=["p2"] + RA, writes=["u2"])
                vtt(p2i[:, :, j + 1], u1[:], u2[:], ALU.add, reads=["u1", "u2"], writes=["p2"])
            vcopy(apr[:, :, 0], p2r[:, :, 16], reads=["p2"], writes=["ap"])
            vcopy(api[:, :, 0], p2i[:, :, 16], reads=["p2"], writes=["ap"])
            for r in range(7):
                vtt(u1[:], apr[:, :, r], apr[:, :, r], ALU.mult, reads=["ap"], writes=["u1"])
                vtt(u2[:], api[:, :, r], api[:, :, r], ALU.mult, reads=["ap"], writes=["u2"])
                vtt(apr[:, :, r + 1], u1[:], u2[:], ALU.subtract, reads=["u1", "u2"], writes=["ap"])
                vtt(u1[:], apr[:, :, r], api[:, :, r], ALU.mult, reads=["ap"], writes=["u1"])
                vts(api[:, :, r + 1], u1[:], 2.0, None, ALU.mult, reads=["u1"], writes=["ap"])
            vts(napi[:], api[:], -1.0, None, ALU.mult, reads=["ap"], writes=["ap"])
            wy_r = sb("wy_r", [P, 16, 16, 32], BF16, S)
            wy_i = sb("wy_i", [P, 16, 16, 32], BF16, S)
            w1 = sb("w1", [P, 16, 16, 16], F32, S)
            w2 = sb("w2", [P, 16, 16, 16], F32, S)
            fw.op("dve", lambda h: h.memset(wy_r[:], 0.0), writes=["wy"])
            fw.op("dve", lambda h: h.memset(wy_i[:], 0.0), writes=["wy"])
            shw = [P, 16, 16, 16]
            crb = c2r[:].unsqueeze(2).to_broadcast(shw)
            cib = c2i[:].unsqueeze(2).to_broadcast(shw)
            prb = p2r[:, :, 1:17].unsqueeze(3).to_broadcast(shw)
            pib = p2i[:, :, 1:17].unsqueeze(3).to_broadcast(shw)
            vtt(w1[:], crb, prb, ALU.mult, reads=["l2in", "p2"], writes=["w1"])
            vtt(w2[:], cib, pib, ALU.mult, reads=["l2in", "p2"], writes=["w2"])
            for e in range(2):
                vtt(wy_r[64 * e:64 * e + 64, :, :, 16 * e:16 * e + 16], w1[64 * e:64 * e + 64],
                    w2[64 * e:64 * e + 64], ALU.subtract, reads=["w1", "w2"], writes=["wy"])
            vtt(w1[:], crb, pib, ALU.mult, reads=["l2in", "p2", "wy"], writes=["w1"])
            vtt(w2[:], cib, prb, ALU.mult, reads=["l2in", "p2", "wy"], writes=["w2"])
            vtt(w1[:], w1[:], w2[:], ALU.add, reads=["w1", "w2"], writes=["w1"])
            for e in range(2):
                vts(wy_i[64 * e:64 * e + 64, :, :, 16 * e:16 * e + 16], w1[64 * e:64 * e + 64], -1.0, None,
                    ALU.mult, reads=["w1"], writes=["wy"])
            fw.dma("sp", wyr_s, wy_r[:].rearrange("p a b c -> p (a b c)"), reads=["wy"], writes=["wys"])
            fw.dma("sp", wyi_s, wy_i[:].rearrange("p a b c -> p (a b c)"), reads=["wy"], writes=["wys"])
            fw.barrier()

        SP0.__exit__(None, None, None)
        with ExitStack() as S:
            csb = sb("csb", [P, NSEQ, NK], F32, S)
            ca = sb("ca", [P, NSEQ, NK], F32, S)
            cbb = sb("cbb", [P, NSEQ, NK, P], BF16, S)
            ba = sb("ba", [P, 6 * D], F32, S)
            wa = [sb("wa%d" % i, [P, NK, 512], BF16, S) for i in range(2)]
            modt = [sb("modt%d" % i, [P, 512], F32, S) for i in range(2)]
            fw.dma("sp", csb[:], cT_d, writes=["csb"])
            fw.dma("sp", ba[:], bada_d.partition_broadcast(P), writes=["ba"])
            act(ca[:], csb[:], AF.Silu, reads=["csb"], writes=["ca"])
            for b in range(NSEQ):
                vcopy(cbb[:, b, :, :], ca[:, b, :].unsqueeze(2).to_broadcast([P, NK, P]),
                      reads=["ca"], writes=["cbb"])
            wv = wadab.rearrange("(k p) n -> p k n", p=P)
            it = 0
            for n in range(12):
                wt = wa[n % 2]
                fw.dma("sp", wt[:], wv[:, :, 512 * n:512 * n + 512], reads=["wadab"], writes=["wa%d" % (n % 2)])
                for b in range(NSEQ):
                    bank = it % 2
                    for k in range(NK):
                        mm(ps[bank][:], cbb[:, b, k, :], wt[:, k, :], k == 0, k == NK - 1,
                           reads=["cbb", "wa%d" % (n % 2)], writes=[PS[bank]], inc=(k == NK - 1))
                    mt = modt[it % 2]
                    one = 1.0 if (n // 2) in (1, 4) else 0.0
                    vstt(mt[:], ps[bank][:], one, ba[:, 512 * n:512 * n + 512], ALU.add, ALU.add,
                         reads=["ba"], writes=[PS[bank], "modt%d" % (it % 2)])
                    fw.dma("sp", mods[b, n // 2, :, 512 * (n % 2):512 * (n % 2) + 512], mt[:],
                           reads=["modt%d" % (it % 2)], writes=["mods"])
                    it += 1
            fw.barrier()

        def ln_stats(S_, xt_ap, xres, tag, st, mv, sd, rstd):
            for hlf in range(2):
                fw.op("dve", lambda h: h.bn_stats(out=st[:, 6 * hlf:6 * hlf + 6],
                                                  in_=xt_ap[:, 512 * hlf:512 * hlf + 512]),
                      reads=[xres], writes=[tag + "st"])
            fw.op("dve", lambda h: h.bn_aggr(out=mv[:], in_=st[:]), reads=[tag + "st"], writes=[tag + "mv"])
            vts(sd[:], mv[:, 1:2], EPS, None, ALU.add, reads=[tag + "mv"], writes=[tag + "sd"])
            vtt(rstd[:], sd[:], mhalf[:, 0:1], ALU.pow, reads=[tag + "sd", "mhalf"], writes=[tag + "rstd"], eng="pool")

        epsb = sb("epsb", [P, 1], F32)
        fw.op("dve", lambda h: h.memset(epsb[:], EPS), writes=["epsb"])

        for s in range(NSEQ):
            with ExitStack() as SQ:
                std = sb("std", [P, 4, TT, NC16], BF16, SQ)
                with ExitStack() as S:
                    ktb = sb("ktb", [P, 4, L], BF16, S)
                    vb = sb("vb", [P, NB, 4, 129], BF16, S)
                    wi = sb("wi", [P, NK, 2048], BF16, S)
                    a1 = sb("a1", [P, D], F32, S)
                    s1 = sb("s1", [P, D], F32, S)
                    xt = [sb("xt%d" % i, [P, D], F32, S) for i in range(2)]
                    ub = sb("ub", [P, D], BF16, S)
                    uts = [sb("ut%d" % i, [P, NK, 512], BF16, S) for i in range(2)]
                    qt = sb("qt", [P, 4, 512], BF16, S)
                    pt = [[sb("pt%d%d" % (m, i), [P, 512], BF16, S) for i in range(2)] for m in range(2)]
                    ost = [sb("ost%d" % i, [P, 4, 512], BF16, S) for i in range(2)]
                    st = sb("st", [P, 12], F32, S)
                    mv = sb("mv", [P, 2], F32, S)
                    sd = sb("sd", [P, 1], F32, S)
                    rstd = sb("rstd", [P, 1], F32, S)
                    osq = sb("osq", [P, P], F32, S)
                    rz = sb("rz", [P, 2], F32, S)
                    oraw = [sb("oraw%d" % i, [P, 258], F32, S) for i in range(3)]
                    osb_all = sb("osb_all", [P, 16, P], F32, S)
                    onb_all = sb("onb_all", [P, 16, P], BF16, S)
                    ss_all = sb("ss_all", [P, 16], F32, S)
                    sd16 = sb("sd16", [P, 16], F32, S)
                    rs16 = sb("rs16", [P, 16], F32, S)
                    fin_n = [0]
                    pending = []
                    dbias = sb("dbias", [P, 4, 2, P], F32, S)
                    fw.dma("sp", dbias[:].rearrange("p a b c -> p (a b c)"), dbias_s, reads=["dbias_s"], writes=["dbias"])
                    fw.dma("sp", wi[:], winb.rearrange("(k p) n -> p k n", p=P), reads=["winb"], writes=["wi"])
                    fw.dma("sp", a1[:], mods[s, 1], reads=["mods"], writes=["a1"])
                    fw.dma("sp", s1[:], mods[s, 0], reads=["mods"], writes=["s1"])
                    fw.op("pool", lambda h: h.memset(vb[:, :, :, 128:129], 1.0), writes=["vb"])
                    def lnf_tile(c, i):
                        xx = xt[i % 2]
                        xr = "xt%d" % (i % 2)
                        tok0 = 512 * c + 128 * i
                        fw.dma("sp", xx[:], x_d[s, tok0:tok0 + 128, :], writes=[xr])
                        ln_stats(S, xx, xr, "a", st, mv, sd, rstd)
                        vstt(xx[:], xx[:], mv[:, 0:1], a1[:], ALU.subtract, ALU.mult,
                             reads=[xr, "amv", "a1"], writes=[xr])
                        vstt(ub[:], xx[:], rstd[:, 0:1], s1[:], ALU.mult, ALU.add,
                             reads=[xr, "arstd", "s1"], writes=["ub"])

                    def lnf_T(c, i):
                        pst = ps[3][:].bitcast(BF16)
                        for k in range(NK):
                            fw.op("pe", lambda h: h.transpose(pst[:, 128 * k:128 * k + 128],
                                                              ub[:, 128 * k:128 * k + 128], identb[:]),
                                  reads=["ub", "identb"], writes=[PS[3]], inc=(k == NK - 1))

                    def lnf_copy(c, i):
                        pst = ps[3][:].bitcast(BF16)
                        act(uts[c % 2][:, :, 128 * i:128 * i + 128], pst.rearrange("p (k t) -> p k t", k=NK), AF.Copy,
                            writes=[PS[3], "ut%d" % (c % 2)])

                    for ci in range(NCH):
                        ut = uts[ci % 2]
                        utn = "ut%d" % (ci % 2)
                        if ci == 0:
                            for i in range(4):
                                lnf_tile(0, i)
                                lnf_T(0, i)
                                lnf_copy(0, i)
                        bi_ = 0
                        for m in range(12):
                            bank = 1 + (bi_ % 3)
                            bi_ += 1
                            col0 = 128 * m if m < 8 else 1536 + 128 * (m - 8)
                            for k in range(NK):
                                mm(ps[bank][:], wi[:, k, col0:col0 + 128], ut[:, k, :], k == 0, k == NK - 1,
                                   reads=["wi", utn], writes=[PS[bank]], inc=(k == NK - 1))
                            if m < 4:
                                act(qt[:, m, :], ps[bank][:], AF.Copy, writes=[PS[bank], "qt"])
                            elif m < 8:
                                vcopy(ktb[:, m - 4, 512 * ci:512 * ci + 512], ps[bank][:],
                                      writes=[PS[bank], "ktb"])
                            else:
                                act(std[:, m - 8, :, 32 * ci:32 * ci + 32],
                                    ps[bank][:].rearrange("p (c t) -> p t c", t=TT), AF.Copy,
                                    writes=[PS[bank], "std"])
                        for i in range(4):
                            bank = 1 + (bi_ % 3)
                            bi_ += 1
                            for k in range(NK):
                                mm(ps[bank][:], ut[:, k, 128 * i:128 * i + 128], wi[:, k, 1024:1536], k == 0,
                                   k == NK - 1, reads=["wi", utn], writes=[PS[bank]], inc=(k == NK - 1))
                            vcopy(vb[:, 4 * ci + i, :, 0:128], ps[bank][:].rearrange("p (h e) -> p h e", h=4),
                                  writes=[PS[bank], "vb"])
                        while pending:
                            pending.pop()()
                        osx = ost[ci % 2]
                        osr = "ost%d" % (ci % 2)
                        cp_pending = []
                        for hh in range(4):
                            nblk = 4 * ci + 4
                            if ci + 1 < NCH:
                                lnf_tile(ci + 1, hh)

                            def qk(j, m):
                                r = j - 4 * ci
                                q0 = 128 * max(r, 0)
                                bank = (2 * j + m) % 3
                                special = [i for i in range(4) if 0 <= i - r <= 1]
                                mm(ps[bank][:, q0:512], ktb[64 * m:64 * m + 64, hh, 128 * j:128 * j + 128],
                                   qt[64 * m:64 * m + 64, hh, q0:512], True, len(special) == 0,
                                   reads=["ktb", "qt"], writes=[PS[bank]], inc=(len(special) == 0))
                                for n_, i in enumerate(special):
                                    last = n_ == len(special) - 1
                                    mm(ps[bank][:, 128 * i:128 * i + 128], identf[:], dbias[:, hh, i - r, :],
                                       False, last, reads=["identf", "dbias"], writes=[PS[bank]], inc=last)

                            qk(0, 0)
                            qk(0, 1)
                            for j in range(nblk):
                                r = j - 4 * ci
                                q0 = 128 * max(r, 0)
                                if j + 1 < nblk:
                                    qk(j + 1, 0)
                                if j == min(1, nblk - 1) and cp_pending:
                                    lnf_copy(*cp_pending.pop())
                                for m in range(2):
                                    bank = (2 * j + m) % 3
                                    pr = "pt%d%d" % (m, j % 2)
                                    act(pt[m][j % 2][:, q0:512], ps[bank][:, q0:512], AF.Exp, scale=0.125,
                                        writes=[PS[bank], pr])
                                    if m == 0 and j + 1 < nblk:
                                        qk(j + 1, 1)
                                for i in range(max(r, 0), 4):
                                    ob = 4 + i
                                    for m in range(2):
                                        pr = "pt%d%d" % (m, j % 2)
                                        lastj = (j == 4 * ci + i)
                                        mm(ps[ob][:, 129 * m:129 * m + 129], pt[m][j % 2][:, 128 * i:128 * i + 128],
                                           vb[:, j, hh, :], (j == 0 and m == 0), lastj,
                                           reads=[pr, "vb"], writes=[PS[ob]], inc=(lastj and m == 1))
                                    if j == 4 * ci + i:
                                        pso = ps[ob]
                                        orw = oraw[fin_n[0] % 3]
                                        orn = "oraw%d" % (fin_n[0] % 3)
                                        fin_n[0] += 1
                                        idx = 4 * hh + i
                                        vcopy(orw[:], pso[:, 0:258], writes=[PS[ob], orn])
                                        fw.op("dve", lambda h: h.reciprocal(
                                            out=rz[:], in_=orw[:, 128:258:129]), reads=[orn], writes=["rz"])
                                        vtt(rz[:, 1:2], rz[:, 1:2], lamneg[:], ALU.mult, reads=["rz", "lamneg"],
                                            writes=["rz"])
                                        vts(osb_all[:, idx, :], orw[:, 0:128], rz[:, 0:1], None, ALU.mult,
                                            reads=["rz", orn], writes=["osb_all"])
                                        vstt(osb_all[:, idx, :], orw[:, 129:257], rz[:, 1:2], osb_all[:, idx, :],
                                             ALU.mult, ALU.add, reads=["rz", orn, "osb_all"], writes=["osb_all"])
                                        vtt(osq[:], osb_all[:, idx, :], osb_all[:, idx, :], ALU.mult,
                                            reads=["osb_all"], writes=["osq"])
                                        fw.op("dve", lambda h: h.tensor_reduce(out=ss_all[:, idx:idx + 1], in_=osq[:],
                                                                               axis=AX.X, op=ALU.add),
                                              reads=["osq"], writes=["ss_all"])
                            if ci + 1 < NCH:
                                lnf_T(ci + 1, hh)
                                cp_pending.append((ci + 1, hh))
                        while cp_pending:
                            lnf_copy(*cp_pending.pop())
                        vts(sd16[:], ss_all[:], 1.0 / 128.0, EPS, ALU.mult, ALU.add, reads=["ss_all"], writes=["sd16"])
                        vtt(rs16[:], sd16[:], mhalf[:], ALU.pow, reads=["sd16", "mhalf"], writes=["rs16"], eng="pool")
                        for idx in range(16):
                            vstt(onb_all[:, idx, :], osb_all[:, idx, :], rs16[:, idx:idx + 1], g08[:], ALU.mult,
                                 ALU.mult, reads=["osb_all", "rs16", "g08"], writes=["onb_all"])

                        def end_fin(ci=ci, osx=osx, osr=osr):
                            for half in range(2):
                                bank = 4 + half
                                pb = ps[bank][:].bitcast(BF16)
                                for n_ in range(8):
                                    idx = 8 * half + n_
                                    fw.op("pe", lambda h: h.transpose(pb[:, 128 * n_:128 * n_ + 128],
                                                                      onb_all[:, idx, :], identb[:]),
                                          reads=["onb_all", "identb"], writes=[PS[bank]], inc=(n_ == 7))
                                vcopy(osx[:, 2 * half:2 * half + 2, :], pb.rearrange("p (h t) -> p h t", h=2),
                                      writes=[PS[bank], osr])
                            for hh in range(4):
                                fw.dma("sp", cat_s[s, hh, :, 512 * ci:512 * ci + 512], osx[:, hh, :], reads=[osr],
                                       writes=["cat"])
                        pending.append(end_fin)
                        if dbg and s == 0 and ci == 0:
                            pending.pop()()
                            dbg_dump("ub", ub[:], [P, D], "ub", BF16)
                            dbg_dump("qt", qt[:], [P, 4, 512], "qt", BF16)
                            dbg_dump("ost", osx[:], [P, 4, 512], osr, BF16)
                    while pending:
                        pending.pop()()
                    if dbg and s == 0:
                        dbg_dump("ktb", ktb[:], [P, 4, L], "ktb", BF16)
                        dbg_dump("vb", vb[:], [P, NB, 4, 129], "vb", BF16)
                        dbg_dump("std", std[:], [P, 4, TT, NC16], "std", BF16)
                    fw.barrier()

                with ExitStack() as S2:
                    with ExitStack() as S:
                        yb = sb("yb", [P, 4, TT, NC16], BF16, S)
                        with ExitStack() as S3:
                            hbr = sb("hbr", [P, 16, NC16 + 1], BF16, S3)
                            hbi = sb("hbi", [P, 16, NC16 + 1], BF16, S3)
                            wgr = sb("wgr", [P, 4, 16, P], BF16, S3)
                            wgi = sb("wgi", [P, 4, 16, P], BF16, S3)
                            wkk = sb("wkk", [P, 4, 16, P], BF16, S3)
                            wyr = sb("wyr", [P, 16, 16, 32], BF16, S3)
                            wyi = sb("wyi", [P, 16, 16, 32], BF16, S3)
                            fw.dma("sp", wgr[:].rearrange("p a b c -> p (a b c)"), wgr_s, reads=["wgs"], writes=["wgr"])
                            fw.dma("sp", wgi[:].rearrange("p a b c -> p (a b c)"), wgi_s, reads=["wgs"], writes=["wgi"])
                            fw.dma("sp", wkk[:].rearrange("p a b c -> p (a b c)"), wk_s, reads=["wks"], writes=["wkk"])
                            fw.dma("sp", wyr[:].rearrange("p a b c -> p (a b c)"), wyr_s, reads=["wys"], writes=["wyr"])
                            fw.dma("sp", wyi[:].rearrange("p a b c -> p (a b c)"), wyi_s, reads=["wys"], writes=["wyi"])
                            fw.op("pool", lambda h: h.memset(hbr[:, :, 0:1], 0.0), writes=["hbr"])
                            fw.op("pool", lambda h: h.memset(hbi[:, :, 0:1], 0.0), writes=["hbi"])
                            kb = [[sb("kb%d%d" % (a_, b_), [P, NC16], F32, S3) for b_ in range(2)] for a_ in range(4)]
                            yt = [sb("yt%d" % i, [P, 512], F32, S3) for i in range(2)]
                            y2 = [sb("y2%d" % i, [P, 512], F32, S3) for i in range(2)]
                            y3 = [sb("y3%d" % i, [P, 512], F32, S3) for i in range(2)]

                            def g_mm(q, set_):
                                t = q // 4
                                r0 = 32 * (q % 4)
                                hr_a, hi_a = kb[2 * set_][0], kb[2 * set_][1]
                                na = "kba%d" % set_
                                for comp, (wg, dst) in enumerate(((wgr, hr_a), (wgi, hi_a))):
                                    for cb_ in range(0, NC16, 512):
                                        w_ = min(512, NC16 - cb_)
                                        bank = 4 + (2 * q + comp) % 4
                                        for tau in range(16):
                                            mm(ps[bank][:, 0:w_], wg[r0:r0 + 32, t, tau, :],
                                               std[r0:r0 + 32, t, tau, cb_:cb_ + w_], tau == 0, tau == 15,
                                               reads=["wgr", "wgi", "std"], writes=[PS[bank]], inc=(tau == 15),
                                               tp=(r0, 0))
                                        act(dst[:, cb_:cb_ + w_], ps[bank][:, 0:w_], AF.Copy, writes=[PS[bank], na])

                            def ks_pair(qs):
                                st_ = []
                                for set_, q in enumerate(qs):
                                    st_.append([(kb[2 * set_][0], kb[2 * set_][1], "kba%d" % set_),
                                                (kb[2 * set_ + 1][0], kb[2 * set_ + 1][1], "kbb%d" % set_)])
                                d_ = 1
                                rnd = 0
                                while d_ < NC16:
                                    n_ = NC16 - d_
                                    for set_, q in enumerate(qs):
                                        (cr_, ci_, cn), (or_, oi_, on) = st_[set_]
                                        arp = apr[:, q, rnd:rnd + 1]
                                        aip = api[:, q, rnd:rnd + 1]
                                        naip = napi[:, q, rnd:rnd + 1]
                                        vcopy(or_[:, 0:d_], cr_[:, 0:d_], reads=[cn], writes=[on], eng="pool")
                                        vcopy(oi_[:, 0:d_], ci_[:, 0:d_], reads=[cn], writes=[on], eng="pool")
                                        vstt(or_[:, d_:], cr_[:, 0:n_], arp, cr_[:, d_:], ALU.mult, ALU.add,
                                             reads=[cn, "ap"], writes=[on])
                                        vstt(oi_[:, d_:], ci_[:, 0:n_], arp, ci_[:, d_:], ALU.mult, ALU.add,
                                             reads=[cn, "ap"], writes=[on])
                                        vstt(or_[:, d_:], ci_[:, 0:n_], naip, or_[:, d_:], ALU.mult, ALU.add,
                                             reads=[cn, on, "ap"], writes=[on])
                                        vstt(oi_[:, d_:], cr_[:, 0:n_], aip, oi_[:, d_:], ALU.mult, ALU.add,
                                             reads=[cn, on, "ap"], writes=[on])
                                        st_[set_].reverse()
                                    d_ *= 2
                                    rnd += 1
                                for set_, q in enumerate(qs):
                                    cr_, ci_, cn = st_[set_][0]
                                    vcopy(hbr[:, q, 1:NC16 + 1], cr_[:], reads=[cn], writes=["hbr"], eng="pool")
                                    vcopy(hbi[:, q, 1:NC16 + 1], ci_[:], reads=[cn], writes=["hbi"], eng="pool")

                            yit = [0]

                            def y_tile(t):
                                for tt in range(TT):
                                    for cb_ in range(0, NC16, 512):
                                        w_ = min(512, NC16 - cb_)
                                        b_ = yit[0] % 2
                                        yit[0] += 1
                                        bi_a, bi_b = 2 * b_, 2 * b_ + 1
                                        for tau in range(tt + 1):
                                            mm(ps[bi_a][:, 0:w_], wkk[:, t, tt - tau, :],
                                               std[:, t, tau, cb_:cb_ + w_], tau == 0, tau == tt,
                                               reads=["wkk", "std"], writes=[PS[bi_a]], inc=(tau == tt))
                                        for qq in range(4):
                                            q = 4 * t + qq
                                            mm(ps[bi_b][32 * qq:32 * qq + 32, 0:w_], wyr[:, q, tt, :],
                                               hbr[:, q, cb_:cb_ + w_], True, False,
                                               reads=["wyr", "hbr"], writes=[PS[bi_b]], inc=False, tp=(0, 32 * qq))
                                            mm(ps[bi_b][32 * qq:32 * qq + 32, 0:w_], wyi[:, q, tt, :],
                                               hbi[:, q, cb_:cb_ + w_], False, True,
                                               reads=["wyi", "hbi"], writes=[PS[bi_b]], inc=(qq == 3), tp=(0, 32 * qq))
                                        ytb, y2b, y3b = yt[b_], y2[b_], y3[b_]
                                        n1, n2, n3 = "yt%d" % b_, "y2%d" % b_, "y3%d" % b_
                                        act(ytb[:, 0:w_], ps[bi_b][:, 0:w_], AF.Copy, writes=[PS[bi_b], n1])
                                        vtt(ytb[:, 0:w_], ytb[:, 0:w_], ps[bi_a][:, 0:w_], ALU.add, reads=[n1],
                                            writes=[PS[bi_a], n1])
                                        vtt(y2b[:, 0:w_], ytb[:, 0:w_], ytb[:, 0:w_], ALU.mult, reads=[n1], writes=[n2])
                                        vts(y2b[:, 0:w_], y2b[:, 0:w_], 0.044715, 1.0, ALU.mult, ALU.add, reads=[n2],
                                            writes=[n2], eng="pool")
                                        vtt(y2b[:, 0:w_], y2b[:, 0:w_], ytb[:, 0:w_], ALU.mult, reads=[n1, n2],
                                            writes=[n2], eng="pool")
                                        act(y3b[:, 0:w_], y2b[:, 0:w_], AF.Sigmoid, scale=2.0 * math.sqrt(2.0 / math.pi),
                                            reads=[n2], writes=[n3])
                                        vtt(yb[:, t, tt, cb_:cb_ + w_], ytb[:, 0:w_], y3b[:, 0:w_], ALU.mult,
                                            reads=[n1, n3], writes=["yb"])

                            for t in range(4):
                                for pr_ in range(2):
                                    qs = (4 * t + 2 * pr_, 4 * t + 2 * pr_ + 1)
                                    g_mm(qs[0], 0)
                                    g_mm(qs[1], 1)
                                    ks_pair(qs)
                                if t >= 1:
                                    y_tile(t - 1)
                            y_tile(3)
                            fw.barrier()
                        with ExitStack() as S3:
                            wgl = sb("wgl", [P, 4, 512], BF16, S3)
                            fw.dma("sp", wgl[:], wglub.rearrange("(k p) n -> p k n", p=P), reads=["wglub"],
                                   writes=["wgl"])
                            sg = [sb("sg%d" % i, [P, 512], F32, S3) for i in range(2)]
                            yf = [sb("yf%d" % i, [P, L], BF16, S3) for i in range(2)]
                            ybf = yb[:].rearrange("p a b c -> p a (b c)")
                            it = 0
                            for m in range(4):
                                yfm = yf[m % 2]
                                yfn = "yf%d" % (m % 2)
                                for cb_ in range(0, L, 512):
                                    b_ = it % 2
                                    it += 1
                                    for k in range(4):
                                        mm(ps[b_][:], wgl[:, k, 128 * m:128 * m + 128], ybf[:, k, cb_:cb_ + 512],
                                           k == 0, k == 3, reads=["wgl", "yb"], writes=[PS[b_]], inc=(k == 3))
                                    act(sg[b_][:], ps[b_][:], AF.Sigmoid, bias=bglu[:, m:m + 1], reads=["bglu"],
                                        writes=[PS[b_], "sg%d" % b_])
                                    ntt = 512 // NC16 if NC16 <= 512 else 1
                                    if NC16 <= 512:
                                        tt0 = cb_ // NC16
                                        outap = yfm[:].rearrange("p (c t) -> p t c", t=TT)[:, tt0:tt0 + ntt, :]
                                        in0 = ybf[:, m, cb_:cb_ + 512].rearrange("p (t c) -> p t c", t=ntt)
                                        in1 = sg[b_][:].rearrange("p (t c) -> p t c", t=ntt)
                                    else:
                                        tt0 = cb_ // NC16
                                        c0 = cb_ % NC16
                                        outap = yfm[:].rearrange("p (c t) -> p t c", t=TT)[:, tt0, c0:c0 + 512]
                                        in0 = ybf[:, m, cb_:cb_ + 512]
                                        in1 = sg[b_][:]
                                    vtt(outap, in0, in1, ALU.mult, reads=["yb", "sg%d" % b_], writes=[yfn])
                                fw.dma("sp", cat_s[s, 4 + m], yfm[:], reads=[yfn], writes=["cat"])
                            fw.barrier()
            with ExitStack() as S:
                wdn = sb("wdn", [P, NJ, D], BF16, S)
                wo = sb("wo", [P, NK, D], BF16, S)
                wu = [sb("wu%d" % i, [P, NK, 256], BF16, S) for i in range(2)]
                g1 = sb("g1", [P, D], F32, S)
                a2 = sb("a2", [P, D], F32, S)
                s2 = sb("s2", [P, D], F32, S)
                g2 = sb("g2", [P, D], F32, S)
                xcs = [sb("xc%d" % i, [P, 4, D], F32, S) for i in range(2)]
                catc = sb("catc", [P, 8, 512], BF16, S)
                tmF = sb("tmF", [P, D], F32, S)
                tmB = tmF
                ubs = [sb("ub2_%d" % i, [P, D], BF16, S) for i in range(2)]
                uts = [sb("ut2_%d" % i, [P, NK, 512], BF16, S) for i in range(2)]
                at = sb("at", [P, NJ, 512], BF16, S)
                hb = [[sb("hb%d%d" % (a_, b_), [P, 514], BF16, S) for b_ in range(2)] for a_ in range(2)]
                sgb = [sb("sgb%d" % a_, [P, 512], F32, S) for a_ in range(2)]
                dg = [[[sb("dg%d%d%d" % (a_, b_, k_), [P, P], BF16, S) for k_ in range(3)] for b_ in range(2)]
                      for a_ in range(2)]
                carry = sb("carry", [P, 44, 2], BF16, S)
                tmpsF = [sb("F" + n_, sh_, F32, S) for n_, sh_ in (("st", [P, 12]), ("mv", [P, 2]), ("sd", [P, 1]),
                                                                  ("rstd", [P, 1]), ("nmr", [P, 1]))]
                tmpsB = [sb("B" + n_, sh_, F32, S) for n_, sh_ in (("st", [P, 12]), ("mv", [P, 2]), ("sd", [P, 1]),
                                                                  ("rstd", [P, 1]), ("nmr", [P, 1]))]
                lngb = sb("lngb", [P, 4, D], F32, S)
                fw.dma("sp", wo[:], woutb.rearrange("(k p) n -> p k n", p=P), reads=["woutb"], writes=["wo"])
                for t_, idx, nm in ((g1, 2, "g1"), (s2, 3, "s2"), (a2, 4, "a2"), (g2, 5, "g2")):
                    fw.dma("sp", t_[:], mods[s, idx], reads=["mods"], writes=[nm])
                for i in range(4):
                    fw.dma("sp", lngb[:, i, :], lng_d[i].partition_broadcast(P), writes=["lngb"])
                fw.op("pool", lambda h: h.memset(carry[:], 0.0), writes=["carry"])
                wcount = [0]
                wissued = {}

                def ln_affine(pf, tmps, tmx, src_ap, srcres, gi, bi, dst_ap, dstres):
                    st, mv, sd, rstd, nmr = tmps
                    ln_stats(S, src_ap, srcres, pf, st, mv, sd, rstd)
                    vstt(tmx[:], src_ap, mv[:, 0:1], lngb[:, gi, :], ALU.subtract, ALU.mult,
                         reads=[srcres, pf + "mv", "lngb"], writes=["Ftm"])
                    vstt(dst_ap, tmx[:], rstd[:, 0:1], lngb[:, bi, :], ALU.mult, ALU.add,
                         reads=["Ftm", pf + "rstd", "lngb"], writes=[dstres])

                def load(c):
                    tok0 = 512 * c
                    fw.dma("sp", xcs[c % 2][:], x_d[s, tok0:tok0 + 512, :].rearrange("(i p) d -> p i d", p=P),
                           writes=["xc%d" % (c % 2)])
                    fw.dma("sp", catc[:], cat_s[s, :, :, tok0:tok0 + 512].rearrange("k p t -> p k t"), reads=["cat"],
                           writes=["catc"])

                def store(c, i):
                    tok0 = 512 * c + 128 * i
                    fw.dma("sp", out_d[s, tok0:tok0 + 128, :], xcs[c % 2][:, i, :],
                           reads=["xc%d" % (c % 2)], writes=["out"])

                FBT = {0: (2, 7), 1: (2, 7), 2: (2, 7), 3: (5, 6)}

                def F_mm(c, i):
                    FB = FBT[i]
                    for hf in range(2):
                        for k in range(NK):
                            mm(ps[FB[hf]][:], catc[:, k, 128 * i:128 * i + 128], wo[:, k, 512 * hf:512 * hf + 512],
                               k == 0, k == NK - 1, reads=["catc", "wo"], writes=[PS[FB[hf]]], inc=(k == NK - 1))

                def F_chain(c, i, piece=None):
                    xc = xcs[c % 2]
                    xr = "xc%d" % (c % 2)
                    ub = ubs[i % 2]
                    ur = "ub2_%d" % (i % 2)
                    st, mv, sd, rstd, nmr = tmpsF
                    FB = FBT[i]
                    if piece in (None, 0):
                        for hf in range(2):
                            sl = slice(512 * hf, 512 * hf + 512)
                            vtt(tmF[:, sl], ps[FB[hf]][:], g1[:, sl], ALU.mult, reads=["g1"],
                                writes=[PS[FB[hf]], "Ftm"])
                        vstt(xc[:, i, :], xc[:, i, :], ALU_ALPHA, tmF[:], ALU.mult, ALU.add, reads=[xr, "Ftm"],
                             writes=[xr])
                    if piece in (None, 1):
                        ln_affine("F", tmpsF, tmF, xc[:, i, :], xr, 0, 1, xc[:, i, :], xr)
                    if piece in (None, 2):
                        ln_stats(S, xc[:, i, :], xr, "F", st, mv, sd, rstd)
                        vstt(tmF[:], xc[:, i, :], mv[:, 0:1], a2[:], ALU.subtract, ALU.mult,
                             reads=[xr, "Fmv", "a2"], writes=["Ftm"])
                        vstt(ub[:], tmF[:], rstd[:, 0:1], s2[:], ALU.mult, ALU.add, reads=["Ftm", "Frstd", "s2"],
                             writes=[ur])

                def F_T(c, i, tb=0):
                    ub = ubs[i % 2]
                    ur = "ub2_%d" % (i % 2)
                    ut = uts[c % 2]
                    pst = ps[tb][:].bitcast(BF16)
                    for k in range(NK):
                        fw.op("pe", lambda h: h.transpose(pst[:, 128 * k:128 * k + 128],
                                                          ub[:, 128 * k:128 * k + 128], identb[:]),
                              reads=[ur, "identb"], writes=[PS[tb]], inc=(k == NK - 1))
                    act(ut[:, :, 128 * i:128 * i + 128], pst.rearrange("p (k t) -> p k t", k=NK), AF.Copy,
                        writes=[PS[tb], "ut2_%d" % (c % 2)])

                def wu_dma(c, j):
                    if (c, j) in wissued or c >= NCH:
                        return
                    n_ = wcount[0]
                    wcount[0] += 1
                    wissued[(c, j)] = n_ % 2
                    fw.dma("sp", wu[n_ % 2][:], wupb2[j], reads=["wupb"], writes=["wu%d" % (n_ % 2)])

                def up(c):
                    prev = None
                    nxt = c + 1 < NCH
                    ut = uts[c % 2]
                    utn = "ut2_%d" % (c % 2)
                    for j in range(NJ):
                        wu_dma(c, j)
                        if j == 2 and nxt:
                            load(c + 1)
                        if nxt:
                            for i_ in range(2):
                                if j == 4 + 3 * i_:
                                    F_mm(c + 1, i_)
                                for pc_ in range(3):
                                    if j == 5 + 3 * i_ + pc_:
                                        F_chain(c + 1, i_, pc_)
                                if j == 11 + 3 * i_:
                                    F_T(c + 1, i_)
                        wb = wissued[(c, j)]
                        wt = wu[wb]
                        wn = "wu%d" % wb
                        b_ = j % 2
                        for vg in range(2):
                            bank = 3 + (2 * j + vg) % 3
                            for k in range(NK):
                                mm(ps[bank][:], wt[:, k, 128 * vg:128 * vg + 128], ut[:, k, :], k == 0, k == NK - 1,
                                   reads=[wn, utn], writes=[PS[bank]], inc=(k == NK - 1))
                            hbb = hb[b_][vg]
                            hh_ = "hbh%d%d" % (b_, vg)
                            hn = "hbb%d%d" % (b_, vg)
                            jj = j + NJ * vg
                            vcopy(hbb[:, 0:2], carry[:, jj, :], reads=["carry"], writes=[hh_], eng="pool")
                            act(hbb[:, 2:514], ps[bank][:], AF.Copy, writes=[PS[bank], hn])
                            vcopy(carry[:, jj, :], hbb[:, 512:514], reads=[hn], writes=["carry"], eng="pool")
                            for k in range(3):
                                vts(dg[b_][vg][k][:], identb[:], cw[:, k, jj:jj + 1], None, ALU.mult,
                                    reads=["identb", "cw"], writes=["dg%d%d" % (b_, vg)])
                        if prev is not None:
                            conv_gate(prev)
                        prev = j
                    conv_gate(prev)

                CB = (6, 1)

                def conv_gate(j):
                    b_ = j % 2
                    for vg in range(2):
                        hbb = hb[b_][vg]
                        hh_ = "hbh%d%d" % (b_, vg)
                        hn = "hbb%d%d" % (b_, vg)
                        for k in range(3):
                            mm(ps[CB[vg]][:], dg[b_][vg][k][:], hbb[:, k:k + 512], k == 0, k == 2,
                               reads=["dg%d%d" % (b_, vg), hh_, hn], writes=[PS[CB[vg]]], inc=(k == 2))
                    jg = j + NJ
                    act(sgb[b_][:], ps[CB[1]][:], AF.Silu, bias=cb[:, jg:jg + 1], reads=["cb"],
                        writes=[PS[CB[1]], "sgb%d" % b_])
                    vstt(at[:, j, :], ps[CB[0]][:], cb[:, j:j + 1], sgb[b_][:], ALU.add, ALU.mult,
                         reads=["cb", "sgb%d" % b_], writes=[PS[CB[0]], "at%d" % j])

                DB = ((0, 1), (3, 4))

                def down_mm(c, i):
                    for hf in range(2):
                        bk = DB[i % 2][hf]
                        for j in range(NJ):
                            mm(ps[bk][:], at[:, j, 128 * i:128 * i + 128], wdn[:, j, 512 * hf:512 * hf + 512],
                               j == 0, j == NJ - 1, reads=["at%d" % j, "wdn"], writes=[PS[bk]], inc=(j == NJ - 1))

                def ln2_chain(c, i):
                    xc = xcs[c % 2]
                    xr = "xc%d" % (c % 2)
                    for hf in range(2):
                        sl = slice(512 * hf, 512 * hf + 512)
                        bk = DB[i % 2][hf]
                        vtt(tmB[:, sl], ps[bk][:], g2[:, sl], ALU.mult, reads=["g2"], writes=[PS[bk], "Ftm"])
                    vstt(xc[:, i, :], xc[:, i, :], ALU_ALPHA, tmB[:], ALU.mult, ALU.add, reads=[xr, "Ftm"], writes=[xr])
                    ln_affine("B", tmpsB, tmB, xc[:, i, :], xr, 2, 3, xc[:, i, :], xr)

                load(0)
                fw.dma("sp", wdn[:], wdnb.rearrange("(j p) n -> p j n", p=P), reads=["wdnb"], writes=["wdn"])
                for i in range(4):
                    F_mm(0, i)
                    F_chain(0, i)
                    F_T(0, i)
                for c in range(NCH):
                    up(c)
                    wu_dma(c + 1, 0)
                    wu_dma(c + 1, 1)
                    nxt_ = c + 1 < NCH
                    for i in range(4):
                        down_mm(c, i)
                        if nxt_ and i == 0:
                            F_mm(c + 1, 2)
                            F_mm(c + 1, 3)
                        if nxt_ and i == 2:
                            F_T(c + 1, 2, tb=2)
                        if nxt_ and i == 3:
                            F_T(c + 1, 3, tb=2)
                        ln2_chain(c, i)
                        store(c, i)
                        if nxt_ and i == 0:
                            F_chain(c + 1, 2)
                        if nxt_ and i == 1:
                            F_chain(c + 1, 3)
                fw.barrier()
        fw.barrier()
    return nc, dbg_outs, fw.rec


ALU_ALPHA = ALPHA


def _host_inputs(inputs, core, NSEQ, L):
    f = lambda a: np.ascontiguousarray(np.asarray(a, dtype=np.float32))
    b0 = core * NSEQ
    x = f(inputs["x"])[b0:b0 + NSEQ, :L]
    c = f(inputs["c"])[b0:b0 + NSEQ]
    m = {}
    m["x"] = f(x)
    m["cT"] = f(c.reshape(NSEQ, NK, P).transpose(2, 0, 1))
    m["w_ada"] = f(inputs["w_ada"][0])
    m["b_ada"] = f(inputs["b_ada"][0])
    m["w_in"] = f(inputs["w_in"][0])
    m["w_out"] = f(inputs["w_out"][0])
    m["w_up"] = f(inputs["w_up"][0])
    m["w_down"] = f(inputs["w_down"][0])
    m["w_glu"] = f(inputs["w_glu"][0])
    m["ln_gb"] = f(np.stack([inputs["ln1_g"][0], inputs["ln1_b"][0], inputs["ln2_g"][0], inputs["ln2_b"][0]]))
    cwv = f(inputs["conv_w"][0])
    m["convw"] = f(cwv.reshape(3, 44, P).transpose(2, 0, 1))
    m["convb"] = f(f(inputs["conv_b"][0]).reshape(44, P).T)
    m["bglu"] = f(f(inputs["b_glu"][0]).reshape(4, P).T)
    m["subg"] = f(inputs["subln_g"][0])
    m["lamv"] = f(np.concatenate([inputs["lam_q1"][0], inputs["lam_k1"][0], inputs["lam_q2"][0], inputs["lam_k2"][0]]))
    m["relb"] = f(np.asarray(inputs["rel_bias"]).reshape(-1))
    are, aim = f(inputs["ssm_a_re"][0]), f(inputs["ssm_a_im"][0])
    ldt = f(inputs["ssm_log_dt"][0])
    bre, bim = f(inputs["ssm_b_re"][0]), f(inputs["ssm_b_im"][0])
    cre, cim = f(inputs["ssm_c_re"][0]), f(inputs["ssm_c_im"][0])
    dsk = f(inputs["ssm_d"][0])
    rep = lambda a: np.repeat(a.reshape(4, 8, 1, -1), 16, axis=2).reshape(4, P, -1).transpose(1, 0, 2)
    m["ar1"] = f(rep(are))
    m["ai1"] = f(rep(aim))
    m["ldt1"] = f(rep(ldt.reshape(32, 1))[:, :, 0])
    m["br1"] = f(bre.transpose(0, 2, 1).reshape(4, P, 64).transpose(1, 0, 2))
    m["bi1"] = f(bim.transpose(0, 2, 1).reshape(4, P, 64).transpose(1, 0, 2))
    m["d1"] = f(dsk.reshape(4, P).T)
    ctr = cre.reshape(4, 8, 16, 64).transpose(3, 0, 1, 2).reshape(64, 4, P)
    cti = cim.reshape(4, 8, 16, 64).transpose(3, 0, 1, 2).reshape(64, 4, P)
    m["ct"] = f(np.concatenate([ctr, cti], axis=0))
    l2 = lambda a: a.reshape(16, 2, 64).transpose(1, 2, 0).reshape(P, 16)
    m["ar2"] = f(l2(are))
    m["ai2"] = f(l2(aim))
    m["ldt2"] = f(l2(np.repeat(ldt.reshape(32, 1), 64, axis=1)))
    m["c2r"] = f(cre.reshape(16, 2, 16, 64).transpose(1, 3, 0, 2).reshape(P, 16, 16))
    m["c2i"] = f(cim.reshape(16, 2, 16, 64).transpose(1, 3, 0, 2).reshape(P, 16, 16))
    m["ident"] = np.eye(P, dtype=np.float32)
    pidx = np.arange(P)
    gp = pidx // 16
    m["mpar"] = f(np.stack([(gp % 2 == 0), (gp % 2 == 1)], axis=1))
    mb = np.zeros((P, 8, 16), np.float32)
    mb[pidx, gp, :] = 1.0
    m["mblk"] = mb
    return m


_CACHE = {}


def kernel(**inputs):
    NSEQ, L, NCORES = 2, 4096, 8
    if "nc" not in _CACHE:
        _CACHE["nc"] = build(L, NSEQ)[0]
    nc = _CACHE["nc"]
    in_maps = [_host_inputs(inputs, c, NSEQ, L) for c in range(NCORES)]
    res = run_bass_kernel_spmd(nc, in_maps, core_ids=list(range(NCORES)))
    outs = [np.asarray(r["out"], dtype=np.float32) for r in res.results]
    return np.concatenate(outs, axis=0)
```
